# Optimizing a Trainium2 kernel written in Bass

```python
import math
import jax
import jax.numpy as jnp
from jax import lax
import numpy as np

D_MODEL = 2048
BATCH = 2
SEQ = 8192
DEPTH = 2

N_META = 16
BLOCK = 128
PAD_FRONT = BLOCK - N_META
GROUP_W = D_MODEL // 4
DIFF_HEADS = 4
DIFF_QK = 64
DIFF_V = GROUP_W // DIFF_HEADS
SB_HEADS = 8
SB_DIM = GROUP_W // SB_HEADS
SSM_CH = GROUP_W
SSM_GROUP = 16
SSM_NG = SSM_CH // SSM_GROUP
SSM_STATE = 64
RET_HEADS = 4
RET_DIM = GROUP_W // RET_HEADS
FFN_HIDDEN = -(-8 * D_MODEL // (3 * 256)) * 256
ROPE_THETA = 500000.0
ROPE_DIM = DIFF_QK // 4
RET_THETA = 10000.0
NEG_INF = -1e30
EPS = 1e-6
IN_SPLITS = (DIFF_HEADS * 2 * DIFF_QK, DIFF_HEADS * 2 * DIFF_QK, DIFF_HEADS * DIFF_V,
             SB_HEADS * SB_DIM, SB_HEADS * SB_DIM, SB_HEADS * SB_DIM,
             SSM_CH,
             RET_HEADS * RET_DIM, RET_HEADS * RET_DIM, RET_HEADS * RET_DIM, RET_HEADS * RET_DIM)
IN_COLS = sum(IN_SPLITS)

kernel_name = 'hybrid_diff_stickbreak_s5_retention_block'


def rms_norm(x, g):
    xf = x.astype(jnp.float32)
    y = xf * lax.rsqrt(jnp.mean(xf * xf, axis=-1, keepdims=True) + EPS)
    return (y * g.astype(jnp.float32)).astype(x.dtype)


def head_layer_norm(x, g):
    xf = x.astype(jnp.float32)
    mu = jnp.mean(xf, axis=-1, keepdims=True)
    xc = xf - mu
    y = xc * lax.rsqrt(jnp.mean(xc * xc, axis=-1, keepdims=True) + EPS)
    return (y * g.astype(jnp.float32)).astype(x.dtype)


def rotary(x, pos, rot_dim, theta):
    half = rot_dim // 2
    inv_freq = 1.0 / (theta ** (jnp.arange(half, dtype=jnp.float32) * (2.0 / rot_dim)))
    ang = pos.astype(jnp.float32)[:, None] * inv_freq[None, :]
    cos = jnp.cos(ang)[None, :, None, :]
    sin = jnp.sin(ang)[None, :, None, :]
    xr = x[..., :rot_dim].astype(jnp.float32)
    x1, x2 = xr[..., :half], xr[..., half:]
    rot = jnp.concatenate([x1 * cos - x2 * sin, x2 * cos + x1 * sin], axis=-1).astype(x.dtype)
    return jnp.concatenate([rot, x[..., rot_dim:]], axis=-1)


def pad_front(t):
    pad = [(0, 0)] * t.ndim
    pad[1] = (PAD_FRONT, 0)
    return jnp.pad(t, pad)


def to_blocks(t):
    b, lp = t.shape[0], t.shape[1]
    t = t.reshape((b, lp // BLOCK, BLOCK) + t.shape[2:])
    return jnp.moveaxis(t, 1, 0)


def from_blocks(t):
    t = jnp.moveaxis(t, 0, 1)
    return t.reshape((t.shape[0], t.shape[1] * t.shape[2]) + t.shape[3:])


def diff_attention(q, k, v, lam):
    lp = q.shape[1]
    kpos = jnp.arange(lp)
    key_ok = kpos >= PAD_FRONT
    kf = k.astype(jnp.float32)
    vf = v.astype(jnp.float32)
    scale = DIFF_QK ** -0.5

    def block(args):
        qb, i = args
        qpos = i * BLOCK + jnp.arange(BLOCK)
        mask = (kpos[None, :] <= qpos[:, None]) & key_ok[None, :]
        s = jnp.einsum('bqhcd,bkhcd->bhcqk', qb.astype(jnp.float32), kf) * scale
        s = jnp.where(mask, s, NEG_INF)
        p = jax.nn.softmax(s, axis=-1)
        w = p[:, :, 0] - lam * p[:, :, 1]
        return jnp.einsum('bhqk,bkhd->bqhd', w, vf).astype(v.dtype)

    out = lax.map(block, (to_blocks(q), jnp.arange(lp // BLOCK)))
    return from_blocks(out)


def stick_breaking_attention(q, k, v):
    lp = q.shape[1]
    kpos = jnp.arange(lp)
    key_ok = kpos >= PAD_FRONT
    kf = k.astype(jnp.float32)
    vf = v.astype(jnp.float32)
    scale = SB_DIM ** -0.5

    def block(args):
        qb, i = args
        qpos = i * BLOCK + jnp.arange(BLOCK)
        mask = (kpos[None, :] < qpos[:, None]) & key_ok[None, :]
        z = jnp.einsum('bqhd,bkhd->bhqk', qb.astype(jnp.float32), kf) * scale
        log_not_break = jnp.where(mask, jax.nn.log_sigmoid(-z), 0.0)
        tail = lax.cumsum(log_not_break, axis=3, reverse=True)
        log_w = jnp.where(mask, z + tail, NEG_INF)
        w = jnp.exp(log_w)
        return jnp.einsum('bhqk,bkhd->bqhd', w, vf).astype(v.dtype)

    out = lax.map(block, (to_blocks(q), jnp.arange(lp // BLOCK)))
    return from_blocks(out)


def complex_linear_combine(e1, e2):
    a1r, a1i, b1r, b1i = e1
    a2r, a2i, b2r, b2i = e2
    ar = a2r * a1r - a2i * a1i
    ai = a2r * a1i + a2i * a1r
    br = a2r * b1r - a2i * b1i + b2r
    bi = a2r * b1i + a2i * b1r + b2i
    return (ar, ai, br, bi)


def s5_mixer(u, a_re, a_im, log_dt, b_re, b_im, c_re, c_im, d, w_glu, b_glu):
    f32 = jnp.float32
    bsz, seq_len = u.shape[0], u.shape[1]
    uf = u.astype(f32).reshape(bsz, seq_len, SSM_NG, SSM_GROUP)
    dt = jnp.exp(log_dt.astype(f32))[:, None]
    ar, ai = a_re.astype(f32), a_im.astype(f32)
    mag = jnp.exp(ar * dt)
    abar_re = mag * jnp.cos(ai * dt)
    abar_im = mag * jnp.sin(ai * dt)
    nr, ni = abar_re - 1.0, abar_im
    den = ar * ar + ai * ai
    fr = (nr * ar + ni * ai) / den
    fi = (ni * ar - nr * ai) / den
    br, bi = b_re.astype(f32), b_im.astype(f32)
    bbar_re = fr[..., None] * br - fi[..., None] * bi
    bbar_im = fr[..., None] * bi + fi[..., None] * br
    bu_re = jnp.einsum('gph,blgh->lbgp', bbar_re, uf)
    bu_im = jnp.einsum('gph,blgh->lbgp', bbar_im, uf)
    a_seq_re = jnp.broadcast_to(abar_re[None, None], (seq_len, 1, SSM_NG, SSM_STATE))
    a_seq_im = jnp.broadcast_to(abar_im[None, None], (seq_len, 1, SSM_NG, SSM_STATE))
    _, _, xr, xi = lax.associative_scan(complex_linear_combine, (a_seq_re, a_seq_im, bu_re, bu_im), axis=0)
    y = (jnp.einsum('ghp,lbgp->blgh', c_re.astype(f32), xr)
         - jnp.einsum('ghp,lbgp->blgh', c_im.astype(f32), xi))
    y = y.reshape(bsz, seq_len, SSM_CH) + d.astype(f32) * u.astype(f32)
    g = jax.nn.gelu(y)
    out = g * jax.nn.sigmoid(g @ w_glu.astype(f32) + b_glu.astype(f32))
    return out.astype(u.dtype)


def retention(q, k, v):
    f32 = jnp.float32
    bsz = q.shape[0]
    qf = q.astype(f32)
    kf = k.astype(f32) * (RET_DIM ** -0.5)
    vf = v.astype(f32)
    log_gamma = jnp.log1p(-jnp.exp2(-5.0 - jnp.arange(RET_HEADS, dtype=f32)))
    idx = jnp.arange(BLOCK, dtype=f32)
    rel = idx[:, None] - idx[None, :]
    dmat = jnp.where(rel >= 0, jnp.exp(log_gamma[:, None, None] * jnp.maximum(rel, 0.0)), 0.0)
    q_decay = jnp.exp(log_gamma[:, None] * (idx + 1.0)).T[None, :, :, None]
    k_decay = jnp.exp(log_gamma[:, None] * (BLOCK - 1.0 - idx))
    chunk_decay = jnp.exp(log_gamma * BLOCK)[None, :, None, None]

    def step(state, inp):
        qc, kc, vc = inp
        s = jnp.einsum('bqhd,bkhd->bhqk', qc, kc) * dmat
        o_inner = jnp.einsum('bhqk,bkhe->bqhe', s, vc)
        o_cross = jnp.einsum('bqhd,bhde->bqhe', qc, state) * q_decay
        state = chunk_decay * state + jnp.einsum('bkhd,bkhe,hk->bhde', kc, vc, k_decay)
        return state, o_inner + o_cross

    state0 = jnp.zeros((bsz, RET_HEADS, RET_DIM, RET_DIM), f32)
    _, out = lax.scan(step, state0, (to_blocks(qf), to_blocks(kf), to_blocks(vf)))
    return from_blocks(out).astype(q.dtype)


def hybrid_layer(x, pos, layer_idx, norm_mix_pre, w_in, lq1, lk1, lq2, lk2, diff_norm, sb_norm,
                 a_re, a_im, log_dt, b_re, b_im, c_re, c_im, ssm_d, w_glu, b_glu, ssm_norm,
                 ret_norm, w_out, norm_mix_post, norm_ffn_pre, w_gate, w_up, w_down, norm_ffn_post):
    f32 = jnp.float32
    bsz, seq_len, _ = x.shape
    h = rms_norm(x, norm_mix_pre)
    proj = h @ w_in
    split_points = [int(s) for s in np.cumsum(IN_SPLITS)[:-1]]
    dq, dk, dv, sq, sk, sv, su, rq, rk, rv, rg = jnp.split(proj, split_points, axis=-1)

    lam_init = 0.8 - 0.6 * math.exp(-0.3 * layer_idx)
    lam = (jnp.exp(jnp.sum(lq1.astype(f32) * lk1.astype(f32)))
           - jnp.exp(jnp.sum(lq2.astype(f32) * lk2.astype(f32))) + lam_init)
    dq = rotary(dq.reshape(bsz, seq_len, DIFF_HEADS * 2, DIFF_QK), pos, ROPE_DIM, ROPE_THETA)
    dk = rotary(dk.reshape(bsz, seq_len, DIFF_HEADS * 2, DIFF_QK), pos, ROPE_DIM, ROPE_THETA)
    dq = dq.reshape(bsz, seq_len, DIFF_HEADS, 2, DIFF_QK)
    dk = dk.reshape(bsz, seq_len, DIFF_HEADS, 2, DIFF_QK)
    dv = dv.reshape(bsz, seq_len, DIFF_HEADS, DIFF_V)
    o_diff = diff_attention(pad_front(dq), pad_front(dk), pad_front(dv), lam)[:, PAD_FRONT:]
    o_diff = (rms_norm(o_diff, diff_norm) * (1.0 - lam_init)).reshape(bsz, seq_len, GROUP_W)

    sq = sq.reshape(bsz, seq_len, SB_HEADS, SB_DIM)
    sk = sk.reshape(bsz, seq_len, SB_HEADS, SB_DIM)
    sv = sv.reshape(bsz, seq_len, SB_HEADS, SB_DIM)
    o_sb = stick_breaking_attention(pad_front(sq), pad_front(sk), pad_front(sv))[:, PAD_FRONT:]
    o_sb = rms_norm(o_sb, sb_norm).reshape(bsz, seq_len, GROUP_W)

    o_ssm = rms_norm(s5_mixer(su, a_re, a_im, log_dt, b_re, b_im, c_re, c_im, ssm_d, w_glu, b_glu), ssm_norm)

    rq = rotary(rq.reshape(bsz, seq_len, RET_HEADS, RET_DIM), pos, RET_DIM, RET_THETA)
    rk = rotary(rk.reshape(bsz, seq_len, RET_HEADS, RET_DIM), pos, RET_DIM, RET_THETA)
    rv = rv.reshape(bsz, seq_len, RET_HEADS, RET_DIM)
    o_ret = retention(pad_front(rq), pad_front(rk), pad_front(rv))[:, PAD_FRONT:]
    o_ret = head_layer_norm(o_ret, ret_norm).reshape(bsz, seq_len, GROUP_W) * jax.nn.silu(rg)

    mix = jnp.concatenate([o_diff, o_sb, o_ssm, o_ret], axis=-1)
    x = x + rms_norm(mix @ w_out, norm_mix_post)

    hf = rms_norm(x, norm_ffn_pre)
    ffn = (jax.nn.silu(hf @ w_gate) * (hf @ w_up)) @ w_down
    return x + rms_norm(ffn, norm_ffn_post)


def setup_inputs(seed: int = 0) -> dict:
    key = jax.random.key(seed)
    ks = jax.random.split(key, 32)
    f32 = jnp.float32

    def nrm(k, shape, scale):
        return jax.random.normal(k, shape, f32) * scale

    def gain(k, shape):
        return 1.0 + 0.02 * jax.random.normal(k, shape, f32)

    n_idx = jnp.arange(SSM_STATE, dtype=f32)
    return {
        'x': nrm(ks[0], (BATCH, SEQ, D_MODEL), 1.0),
        'meta_tokens': nrm(ks[1], (N_META, D_MODEL), 1.0),
        'norm_mix_pre': gain(ks[2], (DEPTH, D_MODEL)),
        'w_in': nrm(ks[3], (DEPTH, D_MODEL, IN_COLS), D_MODEL ** -0.5),
        'diff_lambda_q1': nrm(ks[4], (DEPTH, DIFF_QK), 0.1),
        'diff_lambda_k1': nrm(ks[5], (DEPTH, DIFF_QK), 0.1),
        'diff_lambda_q2': nrm(ks[6], (DEPTH, DIFF_QK), 0.1),
        'diff_lambda_k2': nrm(ks[7], (DEPTH, DIFF_QK), 0.1),
        'diff_norm': gain(ks[8], (DEPTH, DIFF_V)),
        'sb_norm': gain(ks[9], (DEPTH, SB_DIM)),
        'ssm_a_re': -0.5 + nrm(ks[10], (DEPTH, SSM_NG, SSM_STATE), 0.01),
        'ssm_a_im': math.pi * n_idx + nrm(ks[11], (DEPTH, SSM_NG, SSM_STATE), 0.01),
        'ssm_log_dt': jax.random.uniform(ks[12], (DEPTH, SSM_NG), f32, math.log(1e-3), math.log(1e-1)),
        'ssm_b_re': nrm(ks[13], (DEPTH, SSM_NG, SSM_STATE, SSM_GROUP), (2 * SSM_GROUP) ** -0.5),
        'ssm_b_im': nrm(ks[14], (DEPTH, SSM_NG, SSM_STATE, SSM_GROUP), (2 * SSM_GROUP) ** -0.5),
        'ssm_c_re': nrm(ks[15], (DEPTH, SSM_NG, SSM_GROUP, SSM_STATE), (2 * SSM_STATE) ** -0.5),
        'ssm_c_im': nrm(ks[16], (DEPTH, SSM_NG, SSM_GROUP, SSM_STATE), (2 * SSM_STATE) ** -0.5),
        'ssm_d': nrm(ks[17], (DEPTH, SSM_CH), 1.0),
        'ssm_w_glu': nrm(ks[18], (DEPTH, SSM_CH, SSM_CH), SSM_CH ** -0.5),
        'ssm_b_glu': nrm(ks[19], (DEPTH, SSM_CH), 0.01),
        'ssm_norm': gain(ks[20], (DEPTH, SSM_CH)),
        'ret_norm': gain(ks[21], (DEPTH, RET_DIM)),
        'w_out': nrm(ks[22], (DEPTH, D_MODEL, D_MODEL), D_MODEL ** -0.5),
        'norm_mix_post': gain(ks[23], (DEPTH, D_MODEL)),
        'norm_ffn_pre': gain(ks[24], (DEPTH, D_MODEL)),
        'w_ffn_gate': nrm(ks[25], (DEPTH, D_MODEL, FFN_HIDDEN), D_MODEL ** -0.5),
        'w_ffn_up': nrm(ks[26], (DEPTH, D_MODEL, FFN_HIDDEN), D_MODEL ** -0.5),
        'w_ffn_down': nrm(ks[27], (DEPTH, FFN_HIDDEN, D_MODEL), FFN_HIDDEN ** -0.5),
        'norm_ffn_post': gain(ks[28], (DEPTH, D_MODEL)),
    }


def reference(x, meta_tokens, norm_mix_pre, w_in, diff_lambda_q1, diff_lambda_k1, diff_lambda_q2,
              diff_lambda_k2, diff_norm, sb_norm, ssm_a_re, ssm_a_im, ssm_log_dt, ssm_b_re, ssm_b_im,
              ssm_c_re, ssm_c_im, ssm_d, ssm_w_glu, ssm_b_glu, ssm_norm, ret_norm, w_out,
              norm_mix_post, norm_ffn_pre, w_ffn_gate, w_ffn_up, w_ffn_down, norm_ffn_post):
    bsz = x.shape[0]
    meta = jnp.broadcast_to(meta_tokens[None].astype(x.dtype), (bsz, N_META, D_MODEL))
    h = jnp.concatenate([meta, x], axis=1)
    pos = jnp.arange(h.shape[1], dtype=jnp.int32)
    for l in range(DEPTH):
        h = hybrid_layer(h, pos, l, norm_mix_pre[l], w_in[l], diff_lambda_q1[l], diff_lambda_k1[l],
                         diff_lambda_q2[l], diff_lambda_k2[l], diff_norm[l], sb_norm[l],
                         ssm_a_re[l], ssm_a_im[l], ssm_log_dt[l], ssm_b_re[l], ssm_b_im[l],
                         ssm_c_re[l], ssm_c_im[l], ssm_d[l], ssm_w_glu[l], ssm_b_glu[l], ssm_norm[l],
                         ret_norm[l], w_out[l], norm_mix_post[l], norm_ffn_pre[l],
                         w_ffn_gate[l], w_ffn_up[l], w_ffn_down[l], norm_ffn_post[l])
    return h[:, N_META:]
```

```python
from concourse.bass_utils import run_bass_kernel_spmd
import numpy as np
import concourse.bass as bass
import concourse.mybir as mybir
from contextlib import ExitStack

F32 = mybir.dt.float32
BF16 = mybir.dt.bfloat16
AF = mybir.ActivationFunctionType
ALU = mybir.AluOpType


class Buf:
    __slots__ = ("ap", "writers", "readers")

    def __init__(self, ap):
        self.ap = ap
        self.writers = {}
        self.readers = {}

    def __getitem__(self, k):
        return self.ap[k]


class _Rec:
    def __init__(self):
        self.call = None

    def __getattr__(self, name):
        def f(*a, **k):
            self.call = (name, a, k)
            return self
        return f


class Prog:
    ENG = ("pe", "act", "dve", "pool", "sp")
    NDMA = 8

    def __init__(self, nc, es):
        self.nc = nc
        self.es = es
        self.q = {e: [] for e in self.ENG}
        self.sems = {}
        self.cnt = {}
        for e in ("pe", "act", "dve", "pool"):
            self.sems[e] = es.enter_context(nc.semaphore("s_" + e))
            self.cnt[e] = 0
        self.dma_i = {}
        for e in ("sp", "act", "pool"):
            for j in range(self.NDMA):
                k = "d_%s%d" % (e, j)
                self.sems[k] = es.enter_context(nc.semaphore(k))
                self.cnt[k] = 0
            self.dma_i[e] = 0
        self.known = {e: {} for e in self.ENG}
        self.n_wait = 0
        self.n_ins = 0

    def sb(self, name, shape, dtype=F32):
        self.n_alloc = getattr(self, "n_alloc", 0) + 1
        name = "%s_u%d" % (name, self.n_alloc)
        t = self.es.enter_context(self.nc.sbuf_tensor(name, list(shape), dtype))
        return Buf(t.ap() if hasattr(t, "ap") and callable(getattr(t, "ap")) else t)

    def ps(self, name, shape, dtype=F32):
        t = self.es.enter_context(self.nc.psum_tensor(name, list(shape), dtype))
        b = Buf(t.ap() if hasattr(t, "ap") and callable(getattr(t, "ap")) else t)
        self.psum_bufs = getattr(self, "psum_bufs", set())
        self.psum_bufs.add(id(b))
        if not hasattr(self, "psum_all"):
            self.psum_all = Buf(None)
        return b

    def view(self, ap):
        return Buf(ap)

    def _need(self, eng, reads, writes):
        need = {}
        for b in reads:
            for k, v in b.writers.items():
                if need.get(k, 0) < v:
                    need[k] = v
        for b in writes:
            for k, v in b.writers.items():
                if need.get(k, 0) < v:
                    need[k] = v
            for k, v in b.readers.items():
                if need.get(k, 0) < v:
                    need[k] = v
        return need

    def _emit_waits(self, eng, need, skip_self=False):
        kn = self.known[eng]
        for k, v in need.items():
            if skip_self and k == eng:
                continue
            if kn.get(k, 0) >= v:
                continue
            kn[k] = v
            sem = self.sems[k]
            self.q[eng].append(lambda e, sem=sem, v=v: e.wait_ge(sem, v))
            self.n_wait += 1

    def op(self, eng, fn, reads=(), writes=()):
        import os as _os
        if hasattr(self, "psum_bufs"):
            pr = [b for b in reads if id(b) in self.psum_bufs]
            if pr:
                reads = [b for b in reads if id(b) not in self.psum_bufs]
                writes = list(writes) + pr
            if None and (pr or any(id(b) in self.psum_bufs for b in writes)):
                writes = list(writes) + [self.psum_all]
        need = self._need(eng, reads, writes)
        self._emit_waits(eng, need, skip_self=(eng == "pe"))
        self.cnt[eng] += 1
        v = self.cnt[eng]
        sem = self.sems[eng]
        rec = _Rec()
        fn(rec)
        call = rec.call
        self.q[eng].append(lambda e, call=call, sem=sem: getattr(e, call[0])(*call[1], **call[2]).then_inc(sem, 1))
        self.n_ins += 1
        for b in reads:
            if b.readers.get(eng, 0) < v:
                b.readers[eng] = v
        for b in writes:
            b.writers[eng] = v
            b.readers = {}

    def raw(self, eng, fn):
        self.q[eng].append(lambda e, fn=fn: fn(e))

    def dma(self, eng, out_ap, in_ap, reads=(), writes=(), **kw):
        j = self.dma_i[eng] % self.NDMA
        self.dma_i[eng] += 1
        k = "d_%s%d" % (eng, j)
        need = self._need(eng, reads, writes)
        if self.cnt[k] > 0:
            if need.get(k, 0) < self.cnt[k]:
                need[k] = self.cnt[k]
        self._emit_waits(eng, need)
        self.cnt[k] += 16
        v = self.cnt[k]
        sem = self.sems[k]
        self.q[eng].append(
            lambda e, sem=sem, o=out_ap, i=in_ap, kw=kw: e.dma_start(out=o, in_=i, **kw).then_inc(sem, 16))
        self.n_ins += 1
        for b in reads:
            if b.readers.get(k, 0) < v:
                b.readers[k] = v
        for b in writes:
            b.writers[k] = v
            b.readers = {}

    def finish(self, final_bufs):
        need = {}
        for k, c in self.cnt.items():
            if c > 0:
                need[k] = max(need.get(k, 0), c)
        self._emit_waits("sp", need)
        self.flush()

    def flush(self):
        nc = self.nc
        q = self.q
        with nc.Block() as block:
            @block.tensor
            def _(e):
                for f in q["pe"]:
                    f(e)

            @block.scalar
            def _(e):
                for f in q["act"]:
                    f(e)

            @block.vector
            def _(e):
                for f in q["dve"]:
                    f(e)

            @block.gpsimd
            def _(e):
                for f in q["pool"]:
                    f(e)

            @block.sync
            def _(e):
                for f in q["sp"]:
                    f(e)
        self.q = {e: [] for e in self.ENG}


import math
import numpy as np
from contextlib import ExitStack
import concourse.bass as bass
import concourse.mybir as mybir

I32 = mybir.dt.int32
AX = mybir.AxisListType
EPS = 1e-6
C_ONES, C_ID, C_MLE, C_MLT, C_UTRI, C_KIDX, C_RDT, C_RQD = 0, 128, 256, 384, 512, 640, 768, 896
C_PADB, C_KDEC, C_G128, C_DNORM, C_SBNORM, C_RNORM, C_SSMD, C_SGN, C_NSGN = 1024, 1025, 1026, 1027, 1028, 1029, 1030, 1031, 1032
C_H0 = 1033
C_RMASK = 1040
C_LAMV = 1048
C_BLK = C_LAMV + 256
C_ZERO = C_BLK + 128
NCST = C_ZERO + 512
NCOL = 1920
W_DQ, W_DQP, W_DK, W_DKP, W_DV = 0, 128, 256, 384, 512
W_SQ, W_SK, W_SV = 640, 768, 896
W_SU = 1024
W_RQ, W_RQP, W_RK, W_RKP, W_RG, W_RV = 1152, 1280, 1408, 1536, 1664, 1792
TWO_PI_S = 6.28318
import os
LVL = 5


def build_ab(NS, layer_idx, parts=("diff", "sb", "ssm", "ret")):
    NB = NS // 128
    TT = []
    s = 0
    while s < NS:
        w = min(512, NS - s)
        TT.append((s, w))
        s += w
    lam_init = 0.8 - 0.6 * math.exp(-0.3 * layer_idx)
    nc = bass.Bass("TRN2", target_bir_lowering=False)
    xT = nc.dram_tensor("xT", [2048, NS], F32, kind="ExternalInput").ap()
    gpre = nc.dram_tensor("gpre", [128, 16], F32, kind="ExternalInput").ap()
    w_in = nc.dram_tensor("w_in", [16, 128, NCOL], F32, kind="ExternalInput").ap()
    tabs = nc.dram_tensor("tabs", [4, 128, NS], F32, kind="ExternalInput").ap()
    cst_d = nc.dram_tensor("cst", [128, NCST], F32, kind="ExternalInput").ap()
    ssm_sm = nc.dram_tensor("ssm_sm", [128, 3, 8], F32, kind="ExternalInput").ap()
    ssm_gh = nc.dram_tensor("ssm_gh", [128, 5, 64], F32, kind="ExternalInput").ap()
    ssm_c = nc.dram_tensor("ssm_c", [128, 8, 16], F32, kind="ExternalInput").ap()
    mixq = nc.dram_tensor("mixq", [4, 128, NS], F32, kind="ExternalOutput").ap()
    hT = nc.dram_tensor("hT", [16, 128, NS], BF16).ap()
    xTv = xT.rearrange("(kc p) n -> p kc n", p=128)
    hTv = hT.rearrange("kc p n -> p kc n")
    w_inv = w_in.rearrange("kc p c -> p kc c")

    with ExitStack() as es:
        P = Prog(nc, es)
        hT_b = P.view(hT)
        mix_b = P.view(mixq)
        banks = [P.ps("bank%d" % i, [128, 512]) for i in range(8)]
        cst = P.sb("cst_s", [128, NCST])
        cstb = P.sb("cstb", [128, 1024], BF16)
        P.dma("sp", cst[:], cst_d[:, :], writes=[cst])
        P.op("dve", lambda e: e.tensor_copy(out=cstb[:], in_=cst[:, 0:1024]), reads=[cst], writes=[cstb])
        ones_f = cst[:, C_ONES:C_ONES + 128]
        ones_b = cstb[:, C_ONES:C_ONES + 128]
        id_b = cstb[:, C_ID:C_ID + 128]
        id_f = cst[:, C_ID:C_ID + 128]
        mle_b = cstb[:, C_MLE:C_MLE + 128]
        mlt_b = cstb[:, C_MLT:C_MLT + 128]
        utri_t = P.sb("utri_t", [128, 128], BF16)
        P.op("dve", lambda e: e.tensor_copy(out=utri_t[:], in_=cst[:, C_UTRI:C_UTRI + 128]), reads=[cst], writes=[utri_t])
        utri_b = utri_t[:]
        padb = cst[:, C_PADB:C_PADB + 1]

        def ZSRC(ap):
            n = 1
            for d in ap.shape[1:]:
                n *= d
            return cst[0:ap.shape[0], C_ZERO:C_ZERO + n]

        def col(c, n=128):
            return cst[0:n, c:c + 1]

        def barrier():
            need = {k: c for k, c in P.cnt.items() if c > 0}
            for e in P.ENG:
                P._emit_waits(e, dict(need))

        with ExitStack() as es0:
            P.es = es0
            gp = P.sb("gp", [128, 16])
            P.dma("act", gp[:], gpre[:, :], writes=[gp])
            xb = [[P.sb("x%d_%d" % (i, q), [128, 4, 512]) for q in range(4)] for i in range(2)]
            sqb = [P.sb("sq%d" % i, [128, 512]) for i in range(2)]
            sd = P.sb("sd", [128, 512])
            rstd = P.sb("rstd", [128, 512])
            hb = [[P.sb("h%d_%d" % (i, q), [128, 8, 512], BF16) for q in range(2)] for i in range(2)]
            for ti, (s0, w) in enumerate(TT):
                xt = xb[ti % 2]
                for q in range(4):
                    P.dma("sp" if q % 2 == 0 else "act", xt[q][:, :, :w], xTv[:, 4 * q:4 * q + 4, s0:s0 + w], writes=[xt[q]])
                st = banks[2]
                for kc in range(16):
                    sq = sqb[kc % 2]
                    xs = xt[kc // 4]
                    P.op("act", lambda e, sq=sq, xs=xs, kc=kc: e.activation(out=sq[:, :w], in_=xs[:, kc % 4, :w], func=AF.Square),
                         reads=[xs], writes=[sq])
                    P.op("pe", lambda e, sq=sq, kc=kc: e.matmul(st[:, :w], lhsT=ones_f, rhs=sq[:, :w], start=(kc == 0), stop=(kc == 15)),
                         reads=[sq, cst], writes=[st])
                P.op("act", lambda e: e.activation(out=sd[:, :w], in_=st[:, :w], func=AF.Sqrt, scale=1.0 / 2048, bias=EPS),
                     reads=[st], writes=[sd])
                P.op("dve", lambda e: e.reciprocal(out=rstd[:, :w], in_=sd[:, :w]), reads=[sd], writes=[rstd])
                ht = hb[ti % 2]
                for kc in range(16):
                    xs = xt[kc // 4]
                    hh = ht[kc // 8]
                    P.op("dve", lambda e, xs=xs, hh=hh, kc=kc: e.scalar_tensor_tensor(
                        out=hh[:, kc % 8, :w], in0=xs[:, kc % 4, :w], scalar=gp[:, kc:kc + 1], in1=rstd[:, :w],
                        op0=ALU.mult, op1=ALU.mult), reads=[xs, gp, rstd], writes=[hh])
                for q in range(2):
                    P.dma("sp", hTv[:, 8 * q:8 * q + 8, s0:s0 + w], ht[q][:, :, :w], reads=[ht[q]], writes=[hT_b])
            barrier()
            P.flush()
        P.es = es

        cnt = [0]

        def nextbank():
            cnt[0] += 1
            return banks[cnt[0] % 2]

        def load_w(es_, c0, ncols, name):
            wb = P.sb(name, [128, 16, ncols], BF16)
            stg = [P.sb(name + "s%d" % i, [128, ncols]) for i in range(2)]
            for kc in range(16):
                sg = stg[kc % 2]
                P.dma("sp" if kc % 2 == 0 else "act", sg[:], w_inv[:, kc, c0:c0 + ncols], writes=[sg])
                eng = "pool" if kc % 2 == 0 else "dve"
                P.op(eng, lambda e, sg=sg, kc=kc: e.tensor_copy(out=wb[:, kc, :], in_=sg[:]), reads=[sg], writes=[wb])
            return wb

        def inproj(wb, fm_tiles, tm, evac_fm, evac_tm, pre_tile=None):
            hbuf = [[P.sb("hi%d_%d" % (i, q), [128, 8, 512], BF16) for q in range(2)] for i in range(2)]
            for ti, (s0, w) in enumerate(TT):
                ht = hbuf[ti % 2]
                for q in range(2):
                    P.dma("sp", ht[q][:, :, :w], hTv[:, 8 * q:8 * q + 8, s0:s0 + w], reads=[hT_b], writes=[ht[q]])
                if pre_tile is not None:
                    pre_tile(ti, s0, w)
                for ci, c0 in enumerate(fm_tiles):
                    ps = nextbank()
                    for kc in range(16):
                        hh = ht[kc // 8]
                        P.op("pe", lambda e, ps=ps, hh=hh, kc=kc, c0=c0: e.matmul(
                            ps[:, :w], lhsT=wb[:, kc, c0:c0 + 128], rhs=hh[:, kc % 8, :w], start=(kc == 0), stop=(kc == 15)),
                            reads=[wb, hh], writes=[ps])
                    evac_fm(ci, ps, ti, s0, w)
                if tm is not None:
                    t0, tn = tm
                    for b in range(w // 128):
                        ps = nextbank()
                        for kc in range(16):
                            hh = ht[kc // 8]
                            P.op("pe", lambda e, ps=ps, hh=hh, kc=kc, b=b: e.matmul(
                                ps[:, :tn], lhsT=hh[:, kc % 8, b * 128:(b + 1) * 128], rhs=wb[:, kc, t0:t0 + tn],
                                start=(kc == 0), stop=(kc == 15)), reads=[wb, hh], writes=[ps])
                        evac_tm(ps, s0 // 128 + b)

        def rms_finish(src_ap, npart, w, gain_ap, out_stage, tmp_sq, tmp_sd, tmp_r, stbank, src_bufs, blk=False, maskcol=None, div=None):
            nn = div if div is not None else (64 if maskcol is not None else npart)
            P.op("act", lambda e: e.activation(out=tmp_sq[0:npart, :w], in_=src_ap, func=AF.Square), reads=src_bufs, writes=[tmp_sq])
            if maskcol is not None:
                P.op("dve", lambda e: e.tensor_scalar(out=tmp_sq[0:npart, :w], in0=tmp_sq[0:npart, :w], scalar1=maskcol, scalar2=None, op0=ALU.mult),
                     reads=[tmp_sq, cst], writes=[tmp_sq])
            P.op("pe", lambda e: e.matmul(stbank[0:npart, :w], lhsT=cst[0:npart, C_ONES:C_ONES + npart], rhs=tmp_sq[0:npart, :w], start=True, stop=True),
                 reads=[tmp_sq, cst], writes=[stbank])
            P.op("act", lambda e: e.activation(out=tmp_sd[0:npart, :w], in_=stbank[0:npart, :w], func=AF.Sqrt, scale=1.0 / nn, bias=EPS),
                 reads=[stbank], writes=[tmp_sd])
            P.op("dve", lambda e: e.reciprocal(out=tmp_r[0:npart, :w], in_=tmp_sd[0:npart, :w]), reads=[tmp_sd], writes=[tmp_r])
            P.op("dve", lambda e: e.scalar_tensor_tensor(out=out_stage[0:npart, :w], in0=src_ap, scalar=gain_ap, in1=tmp_r[0:npart, :w],
                                                         op0=ALU.mult, op1=ALU.mult), reads=src_bufs + [tmp_r, cst], writes=[out_stage])

        if "diff" in parts:
            with ExitStack() as esd:
                P.es = esd
                wb = load_w(esd, W_DQ, 640, "wd")
                QT = P.sb("dQT", [128, NS], BF16)
                KT = P.sb("dKT", [128, NS], BF16)
                V = P.sb("dV", [128, NB, 128], BF16)
                tc_ = [P.sb("tcd%d" % i, [128, 512]) for i in range(2)]
                ts_ = [P.sb("tsd%d" % i, [128, 512]) for i in range(2)]
                t1 = P.sb("dt1", [128, 512])
                t2 = P.sb("dt2", [128, 512])
                lamt = P.sb("lamt", [128, 128])
                lams = P.sb("lams", [128, 2])
                lame = P.sb("lame", [128, 2])
                neglam = P.sb("neglam", [128, 1])
                gcol = P.sb("gcol", [128, 1])
                P.op("dve", lambda e: e.tensor_tensor(out=lamt[:, 0:64], in0=cst[:, C_LAMV:C_LAMV + 64], in1=cst[:, C_LAMV + 64:C_LAMV + 128], op=ALU.mult), reads=[cst], writes=[lamt])
                P.op("dve", lambda e: e.tensor_tensor(out=lamt[:, 64:128], in0=cst[:, C_LAMV + 128:C_LAMV + 192], in1=cst[:, C_LAMV + 192:C_LAMV + 256], op=ALU.mult), reads=[cst], writes=[lamt])
                P.op("dve", lambda e: e.tensor_reduce(out=lams[:, 0:1], in_=lamt[:, 0:64], axis=AX.X, op=ALU.add), reads=[lamt], writes=[lams])
                P.op("dve", lambda e: e.tensor_reduce(out=lams[:, 1:2], in_=lamt[:, 64:128], axis=AX.X, op=ALU.add), reads=[lamt], writes=[lams])
                P.op("act", lambda e: e.activation(out=lame[:], in_=lams[:], func=AF.Exp), reads=[lams], writes=[lame])
                P.op("dve", lambda e: e.tensor_tensor(out=neglam[:], in0=lame[:, 1:2], in1=lame[:, 0:1], op=ALU.subtract), reads=[lame], writes=[neglam])
                P.op("dve", lambda e: e.tensor_scalar(out=neglam[:], in0=neglam[:], scalar1=-lam_init, scalar2=None, op0=ALU.add), reads=[neglam], writes=[neglam])
                P.op("dve", lambda e: e.tensor_scalar(out=gcol[:], in0=cst[:, C_DNORM:C_DNORM + 1], scalar1=1.0 - lam_init, scalar2=None, op0=ALU.mult), reads=[cst], writes=[gcol])

                def pre_tile(ti, s0, w):
                    P.dma("act", tc_[ti % 2][:, :w], tabs[0, :, s0:s0 + w], writes=[tc_[ti % 2]])
                    P.dma("act", ts_[ti % 2][:, :w], tabs[1, :, s0:s0 + w], writes=[ts_[ti % 2]])

                def evac_fm(ci, ps, ti, s0, w):
                    dst = QT if ci < 2 else KT
                    if ci % 2 == 0:
                        P.op("dve", lambda e: e.tensor_tensor(out=t1[:, :w], in0=ps[:, :w], in1=tc_[ti % 2][:, :w], op=ALU.mult), reads=[ps, tc_[ti % 2]], writes=[t1])
                    else:
                        P.op("dve", lambda e: e.tensor_tensor(out=t2[:, :w], in0=ps[:, :w], in1=ts_[ti % 2][:, :w], op=ALU.mult), reads=[ps, ts_[ti % 2]], writes=[t2])
                        P.op("pool", lambda e: e.tensor_tensor(out=dst[:, s0:s0 + w], in0=t1[:, :w], in1=t2[:, :w], op=ALU.add), reads=[t1, t2], writes=[dst])

                def evac_tm(ps, blk):
                    P.op("act", lambda e: e.activation(out=V[:, blk, :], in_=ps[:, 0:128], func=AF.Copy), reads=[ps], writes=[V])

                inproj(wb, [W_DQ - W_DQ, W_DQP - W_DQ, W_DK - W_DQ, W_DKP - W_DQ], (W_DV - W_DQ, 128), evac_fm, evac_tm, pre_tile)

                pb = [P.sb("dp%d" % i, [128, 512], BF16) for i in range(4)]
                rr = [P.sb("drr%d" % i, [128, 512]) for i in range(2)]
                tt = [P.sb("dtt%d" % i, [128, 512]) for i in range(2)]
                od = P.sb("dod", [128, 512])
                tsq = P.sb("dsq", [128, 512])
                tsd = P.sb("dsd", [128, 512])
                trr = P.sb("dtr", [128, 512])
                ost = [P.sb("dost%d" % i, [128, 512]) for i in range(2)]
                pc = 0
                for gi, (s0, w) in enumerate(TT):
                    nq = w // 128
                    b0 = s0 // 128
                    last = b0 + nq - 1
                    for kb in range(0, b0 + nq):
                        j = kb - b0
                        c0 = max(j, 0) * 128
                        for m in (0, 1):
                            S = banks[pc % 2]
                            Pt = pb[pc % 4]
                            pc += 1
                            o_b, l_b = banks[2 + m], banks[4 + m]
                            P.op("pe", lambda e, S=S, m=m, kb=kb, c0=c0: e.matmul(
                                S[:, c0:w], lhsT=KT[m * 64:(m + 1) * 64, kb * 128:(kb + 1) * 128], rhs=QT[m * 64:(m + 1) * 64, s0 + c0:s0 + w],
                                start=True, stop=True), reads=[KT, QT], writes=[S])
                            P.op("act", lambda e, S=S, Pt=Pt, kb=kb, c0=c0: e.activation(
                                out=Pt[:, c0:w], in_=S[:, c0:w], func=AF.Exp, scale=0.125, bias=(padb if kb == 0 else 0.0)),
                                reads=[S, cst], writes=[Pt])
                            if j >= 0:
                                P.op("pool", lambda e, Pt=Pt, c0=c0: e.tensor_tensor(out=Pt[:, c0:c0 + 128], in0=Pt[:, c0:c0 + 128], in1=mle_b, op=ALU.mult),
                                     reads=[Pt, cstb], writes=[Pt])
                            P.op("pe", lambda e, Pt=Pt, o_b=o_b, kb=kb, c0=c0: e.matmul(
                                o_b[:, c0:w], lhsT=V[:, kb, :], rhs=Pt[:, c0:w], start=(kb == 0), stop=(kb == last)), reads=[V, Pt], writes=[o_b])
                            P.op("pe", lambda e, Pt=Pt, l_b=l_b, kb=kb, c0=c0: e.matmul(
                                l_b[:, c0:w], lhsT=ones_b, rhs=Pt[:, c0:w], start=(kb == 0), stop=(kb == last)), reads=[cstb, Pt], writes=[l_b])
                    for m in (0, 1):
                        P.op("dve", lambda e, m=m: e.reciprocal(out=rr[m][:, :w], in_=banks[4 + m][:, :w]), reads=[banks[4 + m]], writes=[rr[m]])
                        P.op("dve", lambda e, m=m: e.tensor_tensor(out=tt[m][:, :w], in0=banks[2 + m][:, :w], in1=rr[m][:, :w], op=ALU.mult),
                             reads=[banks[2 + m], rr[m]], writes=[tt[m]])
                    P.op("dve", lambda e: e.scalar_tensor_tensor(out=od[:, :w], in0=tt[1][:, :w], scalar=neglam[:, 0:1], in1=tt[0][:, :w], op0=ALU.mult, op1=ALU.add),
                         reads=[tt[0], tt[1], neglam], writes=[od])
                    og = ost[gi % 2]
                    rms_finish(od[:, :w], 128, w, gcol[:, 0:1], og, tsq, tsd, trr, banks[6], [od, gcol])
                    P.dma("sp", mixq[0, :, s0:s0 + w], og[:, :w], reads=[og], writes=[mix_b])
                barrier()
                P.flush()
            P.es = es

        if "sb" in parts:
            with ExitStack() as ess:
                P.es = ess
                wb = load_w(ess, W_SQ, 384, "ws")
                QT = P.sb("sQT", [128, NS], BF16)
                KTh = [P.sb("sKT%d" % i, [128, NS], BF16) for i in range(2)]
                for i in range(2):
                    for (zs0, zw) in TT:
                        P.op("dve" if i == 0 else "pool", lambda e, i=i, zs0=zs0, zw=zw: e.tensor_copy(out=KTh[i][:, zs0:zs0 + zw], in_=cst[:, C_ZERO:C_ZERO + zw]),
                             writes=[KTh[i]])
                V = None
                Vh = [P.sb("sVh%d" % i, [128, NB, 128], BF16) for i in range(2)]
                for i in range(2):
                    vflat = Vh[i][:].rearrange("p a b -> p (a b)")
                    for (zs0, zw) in TT:
                        P.op("dve" if i == 0 else "pool", lambda e, vflat=vflat, zs0=zs0, zw=zw: e.tensor_copy(out=vflat[:, zs0:zs0 + zw], in_=cst[:, C_ZERO:C_ZERO + zw]),
                             writes=[Vh[i]])

                def evac_fm(ci, ps, ti, s0, w):
                    if ci == 0:
                        P.op("act", lambda e: e.activation(out=QT[:, s0:s0 + w], in_=ps[:, :w], func=AF.Copy), reads=[ps], writes=[QT])
                    else:
                        P.op("act", lambda e: e.activation(out=KTh[0][0:64, s0:s0 + w], in_=ps[0:64, :w], func=AF.Copy), reads=[ps], writes=[KTh[0]])
                        P.op("act", lambda e: e.activation(out=KTh[1][64:128, s0:s0 + w], in_=ps[64:128, :w], func=AF.Copy), reads=[ps], writes=[KTh[1]])

                def evac_tm(ps, blk):
                    P.op("dve", lambda e: e.tensor_copy(out=Vh[0][:, blk, 0:64], in_=ps[:, 0:64]), reads=[ps], writes=[Vh[0]])
                    P.op("dve", lambda e: e.tensor_copy(out=Vh[1][:, blk, 64:128], in_=ps[:, 64:128]), reads=[ps], writes=[Vh[1]])

                inproj(wb, [0, 128], (256, 128), evac_fm, evac_tm)
                Rb = [P.sb("sRb%d" % i, [128, 512]) for i in range(2)]
                Ob = [P.sb("sOb%d" % i, [128, 512]) for i in range(2)]
                eb = [P.sb("seb%d" % i, [128, 512]) for i in range(2)]
                cb = [P.sb("scb%d" % i, [128, 512], BF16) for i in range(2)]
                t1b = [P.sb("st1%d" % i, [128, 512]) for i in range(2)]
                wgb = [P.sb("swg%d" % i, [128, 512], BF16) for i in range(2)]
                tsq = P.sb("ssq", [128, 512])
                tsd = P.sb("ssd", [128, 512])
                trr = P.sb("str", [128, 512])
                ost = [P.sb("sost%d" % i, [128, 512]) for i in range(2)]
                pc = 0
                oc = 0
                for gi, (s0, w) in enumerate(TT):
                    nq = w // 128
                    b0 = s0 // 128
                    first = b0 + nq - 1
                    for h in (0, 1):
                        R = Rb[h]
                        o_b = banks[4 + h]
                        P.op("dve", lambda e, R=R: e.tensor_copy(out=R[:], in_=ZSRC(R[:])), writes=[R])
                        Oacc = Ob[h]
                        P.op("dve", lambda e, Oacc=Oacc: e.tensor_copy(out=Oacc[:], in_=ZSRC(Oacc[:])), writes=[Oacc])
                        for kb in range(first, -1, -1):
                            if LVL < 1:
                                continue
                            j = kb - b0
                            c0 = max(j, 0) * 128
                            z = banks[pc % 2]
                            sfx = banks[2]
                            tot = banks[3]
                            ee, cc, t1, wg = eb[pc % 2], cb[pc % 2], t1b[pc % 2], wgb[pc % 2]
                            pc += 1
                            bias = padb if kb == 0 else 0.0
                            hs = slice(h * 64, (h + 1) * 64)
                            P.op("pe", lambda e, z=z, kb=kb, c0=c0, hs=hs: e.matmul(
                                z[:, c0:w], lhsT=KTh[h][:, kb * 128:(kb + 1) * 128], rhs=QT[:, s0 + c0:s0 + w], start=True, stop=True),
                                reads=[KTh[h], QT], writes=[z])
                            P.op("act", lambda e, z=z, ee=ee, c0=c0, bias=bias: e.activation(out=ee[:, c0:w], in_=z[:, c0:w], func=AF.Exp, scale=0.125, bias=bias),
                                 reads=[z, cst], writes=[ee])
                            P.op("act", lambda e, ee=ee, cc=cc, c0=c0: e.activation(out=cc[:, c0:w], in_=ee[:, c0:w], func=AF.Ln, scale=1.0, bias=1.0),
                                 reads=[ee], writes=[cc])
                            if j >= 0:
                                P.op("pool", lambda e, cc=cc, c0=c0: e.tensor_tensor(out=cc[:, c0:c0 + 128], in0=cc[:, c0:c0 + 128], in1=mlt_b, op=ALU.mult),
                                     reads=[cc, cstb], writes=[cc])
                            if None and h == 0 and kb == -1 and gi == -1:
                                dmp = P.sb("dmp", [128, 512])
                                P.op("dve", lambda e, z=z, dmp=dmp: e.tensor_copy(out=dmp[:, c0:w], in_=z[:, c0:w]), reads=[z], writes=[dmp])
                                P.dma("sp", mixq[0, :, c0:w], dmp[:, c0:w], reads=[dmp], writes=[mix_b])
                                P.dma("sp", mixq[2, :, c0:w], ee[:, c0:w], reads=[ee], writes=[mix_b])
                                dmpk = P.sb("dmpk", [128, 256])
                                P.op("dve", lambda e, dmpk=dmpk, kb=kb: e.tensor_copy(out=dmpk[:, 0:128], in_=KTh[h][:, kb * 128:(kb + 1) * 128]), reads=[KTh[h]], writes=[dmpk])
                                P.op("dve", lambda e, dmpk=dmpk: e.tensor_copy(out=dmpk[:, 128:256], in_=QT[:, s0:s0 + 128]), reads=[QT], writes=[dmpk])
                                P.dma("sp", mixq[3, :, 640:896], dmpk[:, :], reads=[dmpk], writes=[mix_b])
                            if LVL < 2:
                                continue
                            if None != "sfx":
                              P.op("pe", lambda e, sfx=sfx, cc=cc, c0=c0: e.matmul(sfx[:, c0:w], lhsT=(V[:, kb, :] if None else (ones_b if None else utri_b)), rhs=cc[:, c0:w], start=True, stop=True),
                                 reads=[cstb, cc], writes=[sfx])
                            if None != "tot":
                              P.op("pe", lambda e, tot=tot, cc=cc, c0=c0: e.matmul(tot[:, c0:w], lhsT=ones_b, rhs=cc[:, c0:w], start=True, stop=True),
                                 reads=[cstb, cc], writes=[tot])
                            if LVL < 3:
                                continue
                            P.op("dve", lambda e, z=z, t1=t1, R=R, c0=c0: e.scalar_tensor_tensor(
                                out=t1[:, c0:w], in0=z[:, c0:w], scalar=0.125, in1=R[:, c0:w], op0=ALU.mult, op1=ALU.subtract), reads=[z, R], writes=[t1])
                            P.op("dve", lambda e, t1=t1, sfx=sfx, c0=c0: e.tensor_tensor(out=t1[:, c0:w], in0=t1[:, c0:w], in1=sfx[:, c0:w], op=ALU.subtract),
                                 reads=[t1, sfx], writes=[t1])
                            if LVL < 4:
                                if kb > 0:
                                    P.op("dve", lambda e, R=R, tot=tot, c0=c0: e.tensor_tensor(out=R[:, c0:w], in0=R[:, c0:w], in1=tot[:, c0:w], op=ALU.add),
                                         reads=[R, tot], writes=[R])
                                continue
                            P.op("act", lambda e, t1=t1, wg=wg, c0=c0, bias=bias: e.activation(out=wg[:, c0:w], in_=t1[:, c0:w], func=AF.Exp, scale=1.0, bias=bias),
                                 reads=[t1, cst], writes=[wg])
                            if j >= 0:
                                P.op("pool", lambda e, wg=wg, c0=c0: e.tensor_tensor(out=wg[:, c0:c0 + 128], in0=wg[:, c0:c0 + 128], in1=mlt_b, op=ALU.mult),
                                     reads=[wg, cstb], writes=[wg])
                            if kb > 0:
                                P.op("dve", lambda e, R=R, tot=tot, c0=c0: e.tensor_tensor(out=R[:, c0:w], in0=R[:, c0:w], in1=tot[:, c0:w], op=ALU.add),
                                     reads=[R, tot], writes=[R])
                            if None and h == 0 and kb == -1 and gi == -1:
                                d1 = P.sb("d1", [128, 512]); d2 = P.sb("d2", [128, 512]); d3 = P.sb("d3", [128, 512]); d4 = P.sb("d4", [128, 512])
                                P.op("dve", lambda e: e.tensor_copy(out=d1[:, c0:w], in_=t1[:, c0:w]), reads=[t1], writes=[d1])
                                P.op("dve", lambda e: e.tensor_copy(out=d2[:, c0:w], in_=wg[:, c0:w]), reads=[wg], writes=[d2])
                                P.op("dve", lambda e: e.tensor_copy(out=d3[:, c0:w], in_=sfx[:, c0:w]), reads=[sfx], writes=[d3])
                                P.op("dve", lambda e: e.tensor_copy(out=d4[:, c0:w], in_=cc[:, c0:w]), reads=[cc], writes=[d4])
                                d5 = P.sb("d5", [128, 128])
                                P.op("dve", lambda e: e.tensor_copy(out=d5[:], in_=V[:, kb, :]), reads=[V], writes=[d5])
                                P.dma("sp", mixq[3, :, 1024:1152], d5[:], reads=[d5], writes=[mix_b])
                                P.dma("sp", mixq[0, :, c0:w], d1[:, c0:w], reads=[d1], writes=[mix_b])
                                P.dma("sp", mixq[2, :, c0:w], d2[:, c0:w], reads=[d2], writes=[mix_b])
                                P.dma("sp", mixq[3, :, c0:w], d3[:, c0:w], reads=[d3], writes=[mix_b])
                                P.dma("sp", mixq[0, :, 512 + c0:512 + w], d4[:, c0:w], reads=[d4], writes=[mix_b])
                            pv = banks[4 + pc % 2]
                            P.op("pe", lambda e, pv=pv, wg=wg, kb=kb, c0=c0: e.matmul(
                                pv[:, c0:w], lhsT=Vh[h][:, kb, :], rhs=wg[:, c0:w], start=True, stop=True), reads=[Vh[h], wg], writes=[pv])
                            P.op("dve", lambda e, pv=pv, c0=c0: e.tensor_tensor(out=Oacc[:, c0:w], in0=pv[:, c0:w], in1=Oacc[:, c0:w], op=ALU.add),
                                 reads=[pv, Oacc], writes=[Oacc])
                            if None and h == 0 and kb == -1 and gi == -1:
                                d6 = P.sb("d6", [128, 512])
                                P.op("dve", lambda e: e.tensor_copy(out=d6[:, :w], in_=o_b[:, :w]), reads=[o_b], writes=[d6])
                                P.dma("sp", mixq[3, :, 0:w], d6[:, :w], reads=[d6], writes=[mix_b])
                        if None and h == 0 and gi == -1:
                            dmp2 = P.sb("dmp2", [128, 512])
                            P.op("dve", lambda e, dmp2=dmp2, Oacc=Oacc: e.tensor_copy(out=dmp2[:, :w], in_=Oacc[:, :w]), reads=[Oacc], writes=[dmp2])
                            P.dma("sp", mixq[3, :, 0:w], dmp2[:, :w], reads=[dmp2], writes=[mix_b])
                            dmp3 = P.sb("dmp3", [128, 512])
                            P.op("dve", lambda e, dmp3=dmp3, R=R: e.tensor_copy(out=dmp3[:, :w], in_=R[:, :w]), reads=[R], writes=[dmp3])
                            P.dma("sp", mixq[3, :, 512:512 + w], dmp3[:, :w], reads=[dmp3], writes=[mix_b])
                        if LVL < 5:
                            continue
                        og = ost[oc % 2]
                        oc += 1
                        rms_finish(Oacc[:, :w], 128, w, cst[:, C_SBNORM:C_SBNORM + 1], og, tsq, tsd, trr, banks[6], [Oacc, cst], div=64)
                        if h == 0:
                            og0 = og
                        else:
                            P.op("pool", lambda e, og=og, og0=og0: e.tensor_tensor(out=og[:, :w], in0=og[:, :w], in1=og0[:, :w], op=ALU.add), reads=[og, og0], writes=[og])
                            P.dma("sp", mixq[1, :, s0:s0 + w], og[:, :w], reads=[og], writes=[mix_b])
                        if None and h == 0 and gi == 0:
                            e1 = P.sb("e1", [128, 512]); e2 = P.sb("e2", [128, 512])
                            P.op("dve", lambda e: e.tensor_copy(out=e1[:, :w], in_=banks[6][:, :w]), reads=[banks[6]], writes=[e1])
                            P.dma("sp", mixq[0, :, 0:w], e1[:, :w], reads=[e1], writes=[mix_b])
                            P.dma("sp", mixq[2, :, 0:w], trr[:, :w], reads=[trr], writes=[mix_b])
                            P.dma("sp", mixq[3, :, 0:w], tsq[:, :w], reads=[tsq], writes=[mix_b])
                            P.dma("sp", mixq[3, :, 512:512 + w], tsd[:, :w], reads=[tsd], writes=[mix_b])
                            P.op("dve", lambda e: e.tensor_copy(out=e2[:, 0:128], in_=cst[:, C_BLK:C_BLK + 128]), reads=[cst], writes=[e2])
                            P.dma("sp", mixq[0, :, 512:640], e2[:, 0:128], reads=[e2], writes=[mix_b])
                barrier()
                P.flush()
            P.es = es

        if "ret" in parts:
            with ExitStack() as esr:
                P.es = esr
                wb = load_w(esr, W_RQ, 768, "wr")
                QT = P.sb("rQT", [128, NS], BF16)
                KT = P.sb("rKT", [128, NS], BF16)
                GT = P.sb("rGT", [128, NS])
                V = P.sb("rV", [128, NB, 128], BF16)
                tc_ = [P.sb("tcr%d" % i, [128, 512]) for i in range(2)]
                ts_ = [P.sb("tsr%d" % i, [128, 512]) for i in range(2)]
                t1 = P.sb("rt1", [128, 512])
                t2 = P.sb("rt2", [128, 512])

                def pre_tile(ti, s0, w):
                    P.dma("act", tc_[ti % 2][:, :w], tabs[2, :, s0:s0 + w], writes=[tc_[ti % 2]])
                    P.dma("act", ts_[ti % 2][:, :w], tabs[3, :, s0:s0 + w], writes=[ts_[ti % 2]])

                def evac_fm(ci, ps, ti, s0, w):
                    if ci == 4:
                        P.op("act", lambda e: e.activation(out=GT[:, s0:s0 + w], in_=ps[:, :w], func=AF.Silu), reads=[ps], writes=[GT])
                        return
                    dst = QT if ci < 2 else KT
                    if ci % 2 == 0:
                        P.op("dve", lambda e: e.tensor_tensor(out=t1[:, :w], in0=ps[:, :w], in1=tc_[ti % 2][:, :w], op=ALU.mult), reads=[ps, tc_[ti % 2]], writes=[t1])
                    else:
                        P.op("dve", lambda e: e.tensor_tensor(out=t2[:, :w], in0=ps[:, :w], in1=ts_[ti % 2][:, :w], op=ALU.mult), reads=[ps, ts_[ti % 2]], writes=[t2])
                        P.op("pool", lambda e: e.tensor_tensor(out=dst[:, s0:s0 + w], in0=t1[:, :w], in1=t2[:, :w], op=ALU.add), reads=[t1, t2], writes=[dst])

                def evac_tm(ps, blk):
                    P.op("act", lambda e: e.activation(out=V[:, blk, :], in_=ps[:, 0:128], func=AF.Copy), reads=[ps], writes=[V])

                inproj(wb, [0, 128, 256, 384, 512], (640, 128), evac_fm, evac_tm, pre_tile)
                Sf = P.sb("rSf", [128, 128])
                Sb = P.sb("rSb", [128, 128], BF16)
                P.op("dve", lambda e: e.tensor_copy(out=Sf[:], in_=ZSRC(Sf[:])), writes=[Sf])
                P.op("dve", lambda e: e.tensor_copy(out=Sb[:], in_=ZSRC(Sb[:])), writes=[Sb])
                SDb = [P.sb("rSD%d" % i, [128, 128], BF16) for i in range(2)]
                Qdb = [P.sb("rQd%d" % i, [128, 128], BF16) for i in range(2)]
                Kdb = [P.sb("rKd%d" % i, [128, 128], BF16) for i in range(2)]
                ocb = P.sb("roc", [128, 512])
                osq = P.sb("rosq", [128, 512])
                mean = P.sb("rmean", [128, 512])
                msq = P.sb("rmsq", [128, 512])
                var = P.sb("rvar", [128, 512])
                rsd = P.sb("rrsd", [128, 512])
                rrs = P.sb("rrrs", [128, 512])
                ost = [P.sb("rost%d" % i, [128, 512]) for i in range(2)]
                rdt = cst[:, C_RDT:C_RDT + 128]
                rqd = cst[:, C_RQD:C_RQD + 128]
                ktb = banks[4]
                ktv = ktb.ap.bitcast(BF16)
                for gi, (s0, w) in enumerate(TT):
                    nq = w // 128
                    o_b = banks[2 + gi % 2]
                    for bi in range(nq):
                        i = s0 // 128 + bi
                        blk = slice(i * 128, (i + 1) * 128)
                        cs = slice(bi * 128, (bi + 1) * 128)
                        St = banks[i % 2]
                        SD, Qd, Kd = SDb[i % 2], Qdb[i % 2], Kdb[i % 2]
                        P.op("pe", lambda e, St=St, blk=blk: e.matmul(St[:, 0:128], lhsT=KT[:, blk], rhs=QT[:, blk], start=True, stop=True), reads=[KT, QT], writes=[St])
                        P.op("dve", lambda e, St=St, SD=SD: e.tensor_tensor(out=SD[:], in0=St[:, 0:128], in1=rdt, op=ALU.mult), reads=[St, cst], writes=[SD])
                        P.op("pool", lambda e, Qd=Qd, blk=blk: e.tensor_tensor(out=Qd[:], in0=QT[:, blk], in1=rqd, op=ALU.mult), reads=[QT, cst], writes=[Qd])
                        P.op("pe", lambda e, SD=SD, i=i, cs=cs: e.matmul(o_b[:, cs], lhsT=V[:, i, :], rhs=SD[:], start=True, stop=False), reads=[V, SD], writes=[o_b])
                        P.op("pe", lambda e, Qd=Qd, cs=cs: e.matmul(o_b[:, cs], lhsT=Sb[:], rhs=Qd[:], start=False, stop=True), reads=[Sb, Qd], writes=[o_b])
                        P.op("pe", lambda e, blk=blk: e.transpose(out=ktv[:, 0:128], in_=KT[:, blk], identity=id_b), reads=[KT, cstb], writes=[ktb])
                        P.op("dve", lambda e, Kd=Kd: e.tensor_scalar(out=Kd[:], in0=ktv[:, 0:128], scalar1=cst[:, C_KDEC:C_KDEC + 1], scalar2=None, op0=ALU.mult),
                             reads=[ktb, cst], writes=[Kd])
                        P.op("pe", lambda e, Kd=Kd, i=i: e.matmul(banks[5][:, 0:128], lhsT=Kd[:], rhs=V[:, i, :], start=True, stop=True), reads=[Kd, V], writes=[banks[5]])
                        P.op("dve", lambda e: e.scalar_tensor_tensor(out=Sf[:], in0=Sf[:], scalar=cst[:, C_G128:C_G128 + 1], in1=banks[5][:, 0:128], op0=ALU.mult, op1=ALU.add),
                             reads=[Sf, cst, banks[5]], writes=[Sf])
                        P.op("pool", lambda e: e.tensor_copy(out=Sb[:], in_=Sf[:]), reads=[Sf], writes=[Sb])
                    P.op("act", lambda e: e.activation(out=ocb[:, :w], in_=o_b[:, :w], func=AF.Copy), reads=[o_b], writes=[ocb])
                    P.op("act", lambda e: e.activation(out=osq[:, :w], in_=o_b[:, :w], func=AF.Square), reads=[o_b], writes=[osq])
                    P.op("pe", lambda e: e.matmul(banks[6][:, :w], lhsT=ones_f, rhs=ocb[:, :w], start=True, stop=True), reads=[cst, ocb], writes=[banks[6]])
                    P.op("pe", lambda e: e.matmul(banks[7][:, :w], lhsT=ones_f, rhs=osq[:, :w], start=True, stop=True), reads=[cst, osq], writes=[banks[7]])
                    P.op("dve", lambda e: e.tensor_scalar(out=mean[:, :w], in0=banks[6][:, :w], scalar1=1.0 / 128, scalar2=None, op0=ALU.mult), reads=[banks[6]], writes=[mean])
                    P.op("pool", lambda e: e.tensor_tensor(out=msq[:, :w], in0=mean[:, :w], in1=mean[:, :w], op=ALU.mult), reads=[mean], writes=[msq])
                    P.op("dve", lambda e: e.scalar_tensor_tensor(out=var[:, :w], in0=banks[7][:, :w], scalar=1.0 / 128, in1=msq[:, :w], op0=ALU.mult, op1=ALU.subtract),
                         reads=[banks[7], msq], writes=[var])
                    P.op("act", lambda e: e.activation(out=rsd[:, :w], in_=var[:, :w], func=AF.Sqrt, scale=1.0, bias=EPS), reads=[var], writes=[rsd])
                    P.op("dve", lambda e: e.reciprocal(out=rrs[:, :w], in_=rsd[:, :w]), reads=[rsd], writes=[rrs])
                    P.op("pool", lambda e: e.tensor_tensor(out=ocb[:, :w], in0=ocb[:, :w], in1=mean[:, :w], op=ALU.subtract), reads=[ocb, mean], writes=[ocb])
                    P.op("dve", lambda e: e.scalar_tensor_tensor(out=ocb[:, :w], in0=ocb[:, :w], scalar=cst[:, C_RNORM:C_RNORM + 1], in1=rrs[:, :w], op0=ALU.mult, op1=ALU.mult),
                         reads=[ocb, cst, rrs], writes=[ocb])
                    og = ost[gi % 2]
                    P.op("pool", lambda e, og=og: e.tensor_tensor(out=og[:, :w], in0=ocb[:, :w], in1=GT[:, s0:s0 + w], op=ALU.mult), reads=[ocb, GT], writes=[og])
                    P.dma("sp", mixq[3, :, s0:s0 + w], og[:, :w], reads=[og], writes=[mix_b])
                barrier()
                P.flush()
            P.es = es

        if "ssm" in parts:
            with ExitStack() as esm:
                P.es = esm
                wb = load_w(esm, W_SU, 128, "wu")
                uT = P.sb("uT", [128, NS])
                uTb = P.sb("uTb", [128, NS], BF16)

                def evac_fm(ci, ps, ti, s0, w):
                    P.op("act", lambda e: e.activation(out=uT[:, s0:s0 + w], in_=ps[:, :w], func=AF.Copy), reads=[ps], writes=[uT])
                    P.op("dve", lambda e: e.tensor_copy(out=uTb[:, s0:s0 + w], in_=ps[:, :w]), reads=[ps], writes=[uTb])

                inproj(wb, [0], None, evac_fm, None)
                kidx = cst[:, C_KIDX:C_KIDX + 128]
                sgn = cst[:, C_SGN:C_SGN + 1]
                nsgn = cst[:, C_NSGN:C_NSGN + 1]

                def sincos(y, n, sinv, cosv, yi, yf, fx, bufs):
                    P.op("dve", lambda e: e.tensor_copy(out=yi[:, :n], in_=y[:, :n]), reads=[y], writes=[yi])
                    P.op("dve", lambda e: e.tensor_copy(out=yf[:, :n], in_=yi[:, :n]), reads=[yi], writes=[yf])
                    P.op("dve", lambda e: e.tensor_tensor(out=y[:, :n], in0=y[:, :n], in1=yf[:, :n], op=ALU.subtract), reads=[y, yf], writes=[y])
                    for dst, shift in ((sinv, 0.0), (cosv, 0.25)):
                        if shift != 0.0:
                            P.op("dve", lambda e: e.tensor_scalar(out=y[:, :n], in0=y[:, :n], scalar1=shift, scalar2=None, op0=ALU.add), reads=[y], writes=[y])
                        P.op("dve", lambda e: e.tensor_scalar(out=fx[:, :n], in0=y[:, :n], scalar1=0.5, scalar2=None, op0=ALU.is_gt), reads=[y], writes=[fx])
                        P.op("dve", lambda e: e.tensor_tensor(out=y[:, :n], in0=y[:, :n], in1=fx[:, :n], op=ALU.subtract), reads=[y, fx], writes=[y])
                        P.op("dve", lambda e: e.tensor_scalar(out=fx[:, :n], in0=y[:, :n], scalar1=-0.5, scalar2=None, op0=ALU.is_lt), reads=[y], writes=[fx])
                        P.op("dve", lambda e: e.tensor_tensor(out=y[:, :n], in0=y[:, :n], in1=fx[:, :n], op=ALU.add), reads=[y, fx], writes=[y])
                        P.op("act", lambda e, dst=dst: e.activation(out=dst[:, :n], in_=y[:, :n], func=AF.Sin, scale=TWO_PI_S), reads=[y], writes=[dst])

                yv = P.sb("m_yv", [128, 128])
                yi = P.sb("m_yi", [128, 128], I32)
                yf = P.sb("m_yf", [128, 128])
                fx = P.sb("m_fx", [128, 128])
                sinv = P.sb("m_sin", [128, 128])
                cosv = P.sb("m_cos", [128, 128])
                magP = P.sb("m_magP", [128, 128])
                magN = P.sb("m_magN", [128, 128])
                tmpA = P.sb("m_tmpA", [128, 128])
                sm = P.sb("m_sm", [128, 3, 8])
                P.dma("act", sm[:], ssm_sm[:, :, :], writes=[sm])
                dts = P.sb("m_dts", [128, 8])
                ardt = P.sb("m_ardt", [128, 8])
                nardt = P.sb("m_nardt", [128, 8])
                th2 = P.sb("m_th2", [128, 8])
                P.op("act", lambda e: e.activation(out=dts[:], in_=sm[:, 2, :], func=AF.Exp), reads=[sm], writes=[dts])
                P.op("dve", lambda e: e.tensor_tensor(out=ardt[:], in0=sm[:, 0, :], in1=dts[:], op=ALU.mult), reads=[sm, dts], writes=[ardt])
                P.op("dve", lambda e: e.tensor_scalar(out=nardt[:], in0=ardt[:], scalar1=-1.0, scalar2=None, op0=ALU.mult), reads=[ardt], writes=[nardt])
                P.op("dve", lambda e: e.tensor_tensor(out=th2[:], in0=sm[:, 1, :], in1=dts[:], op=ALU.mult), reads=[sm, dts], writes=[th2])
                P.op("dve", lambda e: e.tensor_scalar(out=th2[:], in0=th2[:], scalar1=1.0 / (2 * math.pi), scalar2=None, op0=ALU.mult), reads=[th2], writes=[th2])
                Gr = [P.sb("m_Gr%d" % g, [128, 128]) for g in range(8)]
                Gis = [P.sb("m_Gis%d" % g, [128, 128]) for g in range(8)]
                ErT = [P.sb("m_ErT%d" % g, [128, 128]) for g in range(8)]
                EisT = [P.sb("m_EisT%d" % g, [128, 128]) for g in range(8)]
                magPs = [P.sb("m_mP%d" % g, [128, 128]) for g in range(8)]
                magNs = [P.sb("m_mN%d" % g, [128, 128]) for g in range(8)]
                for g in range(8):
                    P.op("act", lambda e, g=g: e.activation(out=magPs[g][:], in_=kidx, func=AF.Exp, scale=ardt[:, g:g + 1]), reads=[cst, ardt], writes=[magPs[g]])
                    P.op("act", lambda e, g=g: e.activation(out=magNs[g][:], in_=kidx, func=AF.Exp, scale=nardt[:, g:g + 1]), reads=[cst, nardt], writes=[magNs[g]])
                gh = P.sb("m_gh", [128, 5, 64])
                P.dma("act", gh[:], ssm_gh[:, :, :], writes=[gh])
                dtg = P.sb("m_dtg", [128, 64])
                lr = P.sb("m_lr", [128, 64])
                mg = P.sb("m_mg", [128, 64])
                P.op("act", lambda e: e.activation(out=dtg[:], in_=gh[:, 2, :], func=AF.Exp), reads=[gh], writes=[dtg])
                P.op("dve", lambda e: e.tensor_tensor(out=lr[:], in0=gh[:, 0, :], in1=dtg[:], op=ALU.mult), reads=[gh, dtg], writes=[lr])
                P.op("act", lambda e: e.activation(out=mg[:], in_=lr[:], func=AF.Exp), reads=[lr], writes=[mg])
                for g in range(8):
                    magP, magN = magPs[g], magNs[g]
                    P.op("dve", lambda e, g=g: e.tensor_scalar(out=yv[:], in0=kidx, scalar1=th2[:, g:g + 1], scalar2=None, op0=ALU.mult), reads=[cst, th2], writes=[yv])
                    sincos(yv, 128, sinv, cosv, yi, yf, fx, None)
                    P.op("dve", lambda e, g=g: e.tensor_tensor(out=Gr[g][:], in0=magP[:], in1=cosv[:], op=ALU.mult), reads=[magP, cosv], writes=[Gr[g]])
                    P.op("dve", lambda e, g=g: e.scalar_tensor_tensor(out=Gis[g][:], in0=sinv[:], scalar=sgn, in1=magP[:], op0=ALU.mult, op1=ALU.mult),
                         reads=[sinv, cst, magP], writes=[Gis[g]])
                    P.op("dve", lambda e: e.tensor_tensor(out=tmpA[:], in0=magN[:], in1=cosv[:], op=ALU.mult), reads=[magN, cosv], writes=[tmpA])
                    P.op("pe", lambda e: e.transpose(out=banks[0][:, 0:128], in_=tmpA[:], identity=id_f), reads=[tmpA, cst], writes=[banks[0]])
                    P.op("act", lambda e, g=g: e.activation(out=ErT[g][:], in_=banks[0][:, 0:128], func=AF.Copy), reads=[banks[0]], writes=[ErT[g]])
                    P.op("dve", lambda e: e.scalar_tensor_tensor(out=tmpA[:], in0=sinv[:], scalar=nsgn, in1=magN[:], op0=ALU.mult, op1=ALU.mult),
                         reads=[sinv, cst, magN], writes=[tmpA])
                    P.op("pe", lambda e: e.transpose(out=banks[1][:, 0:128], in_=tmpA[:], identity=id_f), reads=[tmpA, cst], writes=[banks[1]])
                    P.op("act", lambda e, g=g: e.activation(out=EisT[g][:], in_=banks[1][:, 0:128], func=AF.Copy), reads=[banks[1]], writes=[EisT[g]])
                abr = P.sb("m_abr", [128, 64])
                abi = P.sb("m_abi", [128, 64])
                den = P.sb("m_den", [128, 64])
                tq = P.sb("m_tq", [128, 64])
                fr = P.sb("m_fr", [128, 64])
                fi = P.sb("m_fi", [128, 64])
                BbT = P.sb("m_BbT", [128, 256])
                ar_g, ai_g, bre_g, bim_g = gh[:, 0, :], gh[:, 1, :], gh[:, 3, :], gh[:, 4, :]
                P.op("dve", lambda e: e.tensor_tensor(out=yv[:, 0:64], in0=ai_g, in1=dtg[:], op=ALU.mult), reads=[gh, dtg], writes=[yv])
                P.op("dve", lambda e: e.tensor_scalar(out=yv[:, 0:64], in0=yv[:, 0:64], scalar1=1.0 / (2 * math.pi), scalar2=None, op0=ALU.mult), reads=[yv], writes=[yv])
                sincos(yv, 64, sinv, cosv, yi, yf, fx, None)
                P.op("dve", lambda e: e.tensor_tensor(out=abr[:], in0=mg[:], in1=cosv[:, 0:64], op=ALU.mult), reads=[mg, cosv], writes=[abr])
                P.op("dve", lambda e: e.tensor_tensor(out=abi[:], in0=mg[:], in1=sinv[:, 0:64], op=ALU.mult), reads=[mg, sinv], writes=[abi])
                P.op("dve", lambda e: e.tensor_scalar(out=abr[:], in0=abr[:], scalar1=-1.0, scalar2=None, op0=ALU.add), reads=[abr], writes=[abr])
                P.op("dve", lambda e: e.tensor_tensor(out=den[:], in0=ar_g, in1=ar_g, op=ALU.mult), reads=[gh], writes=[den])
                P.op("dve", lambda e: e.tensor_tensor(out=tq[:], in0=ai_g, in1=ai_g, op=ALU.mult), reads=[gh], writes=[tq])
                P.op("dve", lambda e: e.tensor_tensor(out=den[:], in0=den[:], in1=tq[:], op=ALU.add), reads=[den, tq], writes=[den])
                P.op("dve", lambda e: e.reciprocal(out=den[:], in_=den[:]), reads=[den], writes=[den])
                P.op("dve", lambda e: e.tensor_tensor(out=fr[:], in0=abr[:], in1=ar_g, op=ALU.mult), reads=[abr, gh], writes=[fr])
                P.op("dve", lambda e: e.tensor_tensor(out=tq[:], in0=abi[:], in1=ai_g, op=ALU.mult), reads=[abi, gh], writes=[tq])
                P.op("dve", lambda e: e.tensor_tensor(out=fr[:], in0=fr[:], in1=tq[:], op=ALU.add), reads=[fr, tq], writes=[fr])
                P.op("dve", lambda e: e.tensor_tensor(out=fr[:], in0=fr[:], in1=den[:], op=ALU.mult), reads=[fr, den], writes=[fr])
                P.op("dve", lambda e: e.tensor_tensor(out=fi[:], in0=abi[:], in1=ar_g, op=ALU.mult), reads=[abi, gh], writes=[fi])
                P.op("dve", lambda e: e.tensor_tensor(out=tq[:], in0=abr[:], in1=ai_g, op=ALU.mult), reads=[abr, gh], writes=[tq])
                P.op("dve", lambda e: e.tensor_tensor(out=fi[:], in0=fi[:], in1=tq[:], op=ALU.subtract), reads=[fi, tq], writes=[fi])
                P.op("dve", lambda e: e.tensor_tensor(out=fi[:], in0=fi[:], in1=den[:], op=ALU.mult), reads=[fi, den], writes=[fi])
                P.op("dve", lambda e: e.tensor_tensor(out=BbT[:, 0:64], in0=fr[:], in1=bre_g, op=ALU.mult), reads=[fr, gh], writes=[BbT])
                P.op("dve", lambda e: e.tensor_tensor(out=tq[:], in0=fi[:], in1=bim_g, op=ALU.mult), reads=[fi, gh], writes=[tq])
                P.op("dve", lambda e: e.tensor_tensor(out=BbT[:, 0:64], in0=BbT[:, 0:64], in1=tq[:], op=ALU.subtract), reads=[BbT, tq], writes=[BbT])
                P.op("dve", lambda e: e.tensor_tensor(out=BbT[:, 64:128], in0=fr[:], in1=bim_g, op=ALU.mult), reads=[fr, gh], writes=[BbT])
                P.op("dve", lambda e: e.tensor_tensor(out=tq[:], in0=fi[:], in1=bre_g, op=ALU.mult), reads=[fi, gh], writes=[tq])
                P.op("dve", lambda e: e.tensor_tensor(out=BbT[:, 64:128], in0=BbT[:, 64:128], in1=tq[:], op=ALU.add), reads=[BbT, tq], writes=[BbT])
                P.op("dve", lambda e: e.tensor_copy(out=BbT[:, 128:192], in_=BbT[:, 64:128]), reads=[BbT], writes=[BbT])
                P.op("dve", lambda e: e.tensor_copy(out=BbT[:, 192:256], in_=BbT[:, 0:64]), reads=[BbT], writes=[BbT])
                Bpad = [P.sb("m_Bp%d" % g, [128, 256], BF16) for g in range(8)]
                for g in range(8):
                    P.op("dve", lambda e, g=g: e.tensor_scalar(out=Bpad[g][:], in0=BbT[:], scalar1=cst[:, C_RMASK + g:C_RMASK + g + 1], scalar2=None, op0=ALU.mult),
                         reads=[BbT, cst], writes=[Bpad[g]])
                ct = P.sb("m_ct", [128, 8, 16])
                P.dma("act", ct[:], ssm_c[:, :, :], writes=[ct])
                Cpad = [P.sb("m_Cp%d" % g, [128, 128], BF16) for g in range(8)]
                for g in range(8):
                    P.op("dve", lambda e, g=g: e.tensor_copy(out=Cpad[g][:], in_=ZSRC(Cpad[g][:])), writes=[Cpad[g]])
                    P.op("dve", lambda e, g=g: e.tensor_scalar(out=Cpad[g][:, 16 * g:16 * g + 16], in0=ct[:, g, :], scalar1=nsgn, scalar2=None, op0=ALU.mult),
                         reads=[ct, cst], writes=[Cpad[g]])
                cs_t = P.sb("m_cs", [128, 8])
                csw_t = P.sb("m_csw", [128, 8])
                P.op("dve", lambda e: e.tensor_copy(out=cs_t[:], in_=ZSRC(cs_t[:])), writes=[cs_t])
                P.op("dve", lambda e: e.tensor_copy(out=csw_t[:], in_=ZSRC(csw_t[:])), writes=[csw_t])
                cs = [P.view(cs_t[:, g:g + 1]) for g in range(8)]
                csw = [P.view(csw_t[:, g:g + 1]) for g in range(8)]
                for g in range(8):
                    cs[g].writers = dict(cs_t.writers)
                    csw[g].writers = dict(csw_t.writers)
                t1b = [P.sb("m_t1%d" % i, [128, 128]) for i in range(2)]
                t2b = [P.sb("m_t2%d" % i, [128, 128]) for i in range(2)]
                zbb = [P.sb("m_zb%d" % i, [128, 128], BF16) for i in range(2)]
                zsb = [P.sb("m_zs%d" % i, [128, 128], BF16) for i in range(2)]
                Ab = [P.sb("m_A%d" % i, [128, 128]) for i in range(2)]
                Aswb = [P.sb("m_As%d" % i, [128, 128]) for i in range(2)]
                X1b = [P.sb("m_X1%d" % i, [128, 128]) for i in range(2)]
                X2b = [P.sb("m_X2%d" % i, [128, 128]) for i in range(2)]
                Xbb = [P.sb("m_Xb%d" % i, [128, 128], BF16) for i in range(2)]
                u1 = [P.sb("m_u1%d" % i, [128, 1]) for i in range(2)]
                u2 = [P.sb("m_u2%d" % i, [128, 1]) for i in range(2)]
                ost = [P.sb("m_ost%d" % i, [128, 512]) for i in range(2)]
                k = 0
                for gi, (s0, w) in enumerate(TT):
                    nq = w // 128
                    Y = banks[4 + gi % 2]
                    for bi in range(nq):
                        i = s0 // 128 + bi
                        blk = slice(i * 128, (i + 1) * 128)
                        ccs = slice(bi * 128, (bi + 1) * 128)
                        for g in range(8):
                            BU = banks[k % 2]
                            CUM = banks[2 + k % 2]
                            t1, t2, zb, A, Asw, X1, X2, Xb, uu1, uu2 = t1b[k % 2], t2b[k % 2], zbb[k % 2], Ab[k % 2], Aswb[k % 2], X1b[k % 2], X2b[k % 2], Xbb[k % 2], u1[k % 2], u2[k % 2]
                            k += 1
                            P.op("pe", lambda e, BU=BU, blk=blk, g=g: e.matmul(BU[:, 0:256], lhsT=uTb[:, blk], rhs=Bpad[g][:], start=True, stop=True), reads=[uTb, Bpad[g]], writes=[BU])
                            P.op("dve", lambda e, BU=BU, t1=t1, g=g: e.tensor_tensor(out=t1[:], in0=BU[:, 0:128], in1=ErT[g][:], op=ALU.mult), reads=[BU, ErT[g]], writes=[t1])
                            P.op("dve", lambda e, BU=BU, t2=t2, g=g: e.tensor_tensor(out=t2[:], in0=BU[:, 128:256], in1=EisT[g][:], op=ALU.mult), reads=[BU, EisT[g]], writes=[t2])
                            P.op("pool", lambda e, t1=t1, t2=t2, zb=zb: e.tensor_tensor(out=zb[:], in0=t1[:], in1=t2[:], op=ALU.add), reads=[t1, t2], writes=[zb])
                            P.op("pe", lambda e, CUM=CUM, zb=zb: e.matmul(CUM[:, 0:128], lhsT=zb[:], rhs=mle_b, start=True, stop=True), reads=[zb, cstb], writes=[CUM])
                            zs = zsb[k % 2]
                            P.op("pool", lambda e, zb=zb, zs=zs: e.tensor_copy(out=zs[:, 0:64], in_=zb[:, 64:128]), reads=[zb], writes=[zs])
                            P.op("pool", lambda e, zb=zb, zs=zs: e.tensor_copy(out=zs[:, 64:128], in_=zb[:, 0:64]), reads=[zb], writes=[zs])
                            P.op("pe", lambda e, CUM=CUM, zs=zs: e.matmul(CUM[:, 128:256], lhsT=zs[:], rhs=mle_b, start=True, stop=True), reads=[zs, cstb], writes=[CUM])
                            P.op("dve", lambda e, CUM=CUM, A=A, g=g: e.tensor_scalar(out=A[:], in0=CUM[:, 0:128], scalar1=cs[g][:, 0:1], scalar2=None, op0=ALU.add), reads=[CUM, cs[g]], writes=[A])
                            P.op("dve", lambda e, CUM=CUM, Asw=Asw, g=g: e.tensor_scalar(out=Asw[:], in0=CUM[:, 128:256], scalar1=csw[g][:, 0:1], scalar2=None, op0=ALU.add), reads=[CUM, csw[g]], writes=[Asw])
                            P.op("pool", lambda e, A=A, X1=X1, g=g: e.tensor_tensor(out=X1[:], in0=A[:], in1=Gr[g][:], op=ALU.mult), reads=[A, Gr[g]], writes=[X1])
                            P.op("pool", lambda e, Asw=Asw, X2=X2, g=g: e.tensor_tensor(out=X2[:], in0=Asw[:], in1=Gis[g][:], op=ALU.mult), reads=[Asw, Gis[g]], writes=[X2])
                            P.op("pool", lambda e, X1=X1, X2=X2, Xb=Xb: e.tensor_tensor(out=Xb[:], in0=X1[:], in1=X2[:], op=ALU.add), reads=[X1, X2], writes=[Xb])
                            P.op("dve", lambda e, X1=X1, X2=X2, g=g: e.tensor_tensor(out=cs[g][:, 0:1], in0=X1[:, 127:128], in1=X2[:, 127:128], op=ALU.add), reads=[X1, X2], writes=[cs[g]])
                            P.op("dve", lambda e, Asw=Asw, uu1=uu1, g=g: e.tensor_tensor(out=uu1[:], in0=Asw[:, 127:128], in1=Gr[g][:, 127:128], op=ALU.mult), reads=[Asw, Gr[g]], writes=[uu1])
                            P.op("dve", lambda e, A=A, uu2=uu2, g=g: e.tensor_tensor(out=uu2[:], in0=A[:, 127:128], in1=Gis[g][:, 127:128], op=ALU.mult), reads=[A, Gis[g]], writes=[uu2])
                            P.op("dve", lambda e, uu1=uu1, uu2=uu2, g=g: e.tensor_tensor(out=csw[g][:, 0:1], in0=uu1[:], in1=uu2[:], op=ALU.subtract), reads=[uu1, uu2], writes=[csw[g]])
                            P.op("pe", lambda e, Xb=Xb, g=g, ccs=ccs: e.matmul(Y[:, ccs], lhsT=Cpad[g][:], rhs=Xb[:], start=(g == 0), stop=(g == 7)), reads=[Cpad[g], Xb], writes=[Y])
                    og = ost[gi % 2]
                    P.op("dve", lambda e, og=og, Y=Y: e.scalar_tensor_tensor(out=og[:, :w], in0=uT[:, s0:s0 + w], scalar=cst[:, C_SSMD:C_SSMD + 1], in1=Y[:, :w], op0=ALU.mult, op1=ALU.add),
                         reads=[uT, cst, Y], writes=[og])
                    P.dma("sp", mixq[2, :, s0:s0 + w], og[:, :w], reads=[og], writes=[mix_b])
                barrier()
                P.flush()
            P.es = es
        P.finish([mix_b])
        print("AB program: ins=%d waits=%d" % (P.n_ins, P.n_wait))
    return nc


import math
import numpy as np

PADF = 112
SPL = np.cumsum([0, 512, 512, 512, 512, 512, 512, 512, 512, 512, 512, 512])
O_DQ, O_DK, O_DV, O_SQ, O_SK, O_SV, O_SU, O_RQ, O_RK, O_RV, O_RG = [int(v) for v in SPL[:11]]


def _perm_diff():
    p = np.arange(128)
    for m in range(2):
        for i in range(8):
            p[m * 64 + i] = m * 64 + i + 8
            p[m * 64 + i + 8] = m * 64 + i
    return p


def _perm_ret():
    return np.concatenate([np.arange(64, 128), np.arange(0, 64)])


def make_tabs(NS):
    pos = (np.arange(NS) - PADF).astype(np.float64)
    pos[pos < 0] = 0
    tabs = np.zeros((4, 128, NS), np.float32)
    invf = 1.0 / (500000.0 ** (np.arange(8) * (2.0 / 16)))
    ang = pos[None, :] * invf[:, None]
    cd = np.ones((64, NS)); sd = np.zeros((64, NS))
    cd[0:8] = np.cos(ang); cd[8:16] = np.cos(ang)
    sd[0:8] = -np.sin(ang); sd[8:16] = np.sin(ang)
    tabs[0] = np.concatenate([cd, cd], 0)
    tabs[1] = np.concatenate([sd, sd], 0)
    invf = 1.0 / (10000.0 ** (np.arange(64) * (2.0 / 128)))
    ang = pos[None, :] * invf[:, None]
    tabs[2] = np.concatenate([np.cos(ang), np.cos(ang)], 0)
    tabs[3] = np.concatenate([-np.sin(ang), np.sin(ang)], 0)
    return tabs


def make_cst(inp, l, qtr):
    c = np.zeros((128, NCST), np.float32)
    i = np.arange(128)
    c[:, C_ONES:C_ONES + 128] = 1.0
    c[:, C_ID:C_ID + 128] = np.eye(128)
    c[:, C_MLE:C_MLE + 128] = (i[None, :] >= i[:, None])
    c[:, C_MLT:C_MLT + 128] = (i[None, :] > i[:, None])
    c[:, C_UTRI:C_UTRI + 128] = np.where(i[:, None] >= i[None, :], 1.0, 0.0)
    c[:, C_KIDX:C_KIDX + 128] = (i + 1)[None, :]
    gam = 1.0 - 2.0 ** (-5.0 - qtr)
    lg = math.log1p(-(2.0 ** (-5.0 - qtr)))
    scale = 128 ** -0.5
    rel = (i[None, :] - i[:, None]).astype(np.float64)
    c[:, C_RDT:C_RDT + 128] = np.where(rel >= 0, np.exp(lg * np.maximum(rel, 0)), 0.0) * scale
    c[:, C_RQD:C_RQD + 128] = np.exp(lg * (i + 1.0))[None, :]
    c[:, C_PADB] = np.where(i < PADF, -1e30, 0.0)
    c[:, C_KDEC] = np.exp(lg * (127.0 - i)) * scale
    c[:, C_G128] = math.exp(lg * 128)
    c[:, C_DNORM] = inp['diff_norm'][l]
    c[:, C_SBNORM] = np.tile(inp['sb_norm'][l], 2)
    c[:, C_RNORM] = inp['ret_norm'][l]
    c[:, C_SSMD] = inp['ssm_d'][l][qtr * 128:(qtr + 1) * 128]
    c[:, C_SGN] = np.where(i < 64, -1.0, 1.0)
    c[:, C_NSGN] = np.where(i < 64, 1.0, -1.0)
    c[:, C_H0] = (i < 64)
    c[:, C_H0 + 1] = (i >= 64)
    for g in range(8):
        c[:, C_RMASK + g] = (i // 16 == g)
    c[:, C_BLK:C_BLK + 128] = (i[:, None] // 64 == i[None, :] // 64)
    lamv = np.concatenate([inp['diff_lambda_q1'][l], inp['diff_lambda_k1'][l], inp['diff_lambda_q2'][l], inp['diff_lambda_k2'][l]])
    c[:, C_LAMV:C_LAMV + 256] = lamv[None, :]
    return c


def make_w_in(inp, l, qtr):
    W = inp['w_in'][l]
    q = qtr
    pd, pr = _perm_diff(), _perm_ret()
    dq = W[:, O_DQ + q * 128:O_DQ + (q + 1) * 128]
    dk = W[:, O_DK + q * 128:O_DK + (q + 1) * 128]
    dv = W[:, O_DV + q * 128:O_DV + (q + 1) * 128]
    sq = W[:, O_SQ + q * 128:O_SQ + (q + 1) * 128]
    sk = W[:, O_SK + q * 128:O_SK + (q + 1) * 128]
    sv = W[:, O_SV + q * 128:O_SV + (q + 1) * 128]
    su = W[:, O_SU + q * 128:O_SU + (q + 1) * 128]
    rq = W[:, O_RQ + q * 128:O_RQ + (q + 1) * 128]
    rk = W[:, O_RK + q * 128:O_RK + (q + 1) * 128]
    rv = W[:, O_RV + q * 128:O_RV + (q + 1) * 128]
    rg = W[:, O_RG + q * 128:O_RG + (q + 1) * 128]
    Wc = np.concatenate([dq, dq[:, pd], dk, dk[:, pd], dv, sq, sk, sv, su, rq, rq[:, pr], rk, rk[:, pr], rg, rv], axis=1)
    assert Wc.shape[1] == NCOL
    return np.ascontiguousarray(Wc.reshape(16, 128, NCOL))


def make_ssm(inp, l, qtr):
    gs = slice(8 * qtr, 8 * qtr + 8)
    ar, ai, ld = inp['ssm_a_re'][l][gs], inp['ssm_a_im'][l][gs], inp['ssm_log_dt'][l][gs]
    bre, bim = inp['ssm_b_re'][l][gs], inp['ssm_b_im'][l][gs]
    cre, cim = inp['ssm_c_re'][l][gs], inp['ssm_c_im'][l][gs]
    sm = np.zeros((128, 3, 8), np.float32)
    sm[:, 0, :] = np.concatenate([ar.T, ar.T], 0)
    sm[:, 1, :] = np.concatenate([ai.T, ai.T], 0)
    sm[:, 2, :] = ld[None, :]
    gh = np.zeros((128, 5, 64), np.float32)
    gh[:, 0, :] = np.repeat(ar, 16, axis=0)
    gh[:, 1, :] = np.repeat(ai, 16, axis=0)
    gh[:, 2, :] = np.repeat(ld, 16)[:, None]
    gh[:, 3, :] = bre.transpose(0, 2, 1).reshape(128, 64)
    gh[:, 4, :] = bim.transpose(0, 2, 1).reshape(128, 64)
    cc = np.zeros((128, 8, 16), np.float32)
    cc[0:64] = cre.transpose(2, 0, 1)
    cc[64:128] = cim.transpose(2, 0, 1)
    return sm, gh, cc


def prep_ab(inp, l, b, qtr, hin, NS, tabs):
    Lt = hin.shape[1]
    xT = np.zeros((2048, NS), np.float32)
    xT[:, PADF:PADF + Lt] = hin[b].T
    sm, gh, cc = make_ssm(inp, l, qtr)
    return {
        "xT": xT,
        "gpre": np.ascontiguousarray(inp['norm_mix_pre'][l].reshape(16, 128).T),
        "w_in": make_w_in(inp, l, qtr),
        "tabs": tabs,
        "cst": make_cst(inp, l, qtr),
        "ssm_sm": sm, "ssm_gh": gh, "ssm_c": cc,
    }


import math
import numpy as np
from contextlib import ExitStack
import concourse.bass as bass
import concourse.mybir as mybir

EPS_C = 1e-6
V_GPOST, V_GFPRE, V_GFPOST, V_SSMN, V_BGLU, V_ONES = 0, 16, 32, 48, 52, 56
NVEC = V_ONES + 128
FH = 5632
NJ = FH // 128


def build_c(tiles):
    NT = sum(w for _, w in tiles)
    nc = bass.Bass("TRN2", target_bir_lowering=False)
    mixT = nc.dram_tensor("mixT", [16, 128, NT], F32, kind="ExternalInput").ap()
    xT = nc.dram_tensor("xT", [16, 128, NT], F32, kind="ExternalInput").ap()
    w_out = nc.dram_tensor("w_out", [16, 128, 16 * 128], F32, kind="ExternalInput").ap()
    w_glu = nc.dram_tensor("w_glu", [128, 4 * 512], F32, kind="ExternalInput").ap()
    w_gate = nc.dram_tensor("w_gate", [NJ, 128, 16 * 128], F32, kind="ExternalInput").ap()
    w_up = nc.dram_tensor("w_up", [NJ, 128, 16 * 128], F32, kind="ExternalInput").ap()
    w_down = nc.dram_tensor("w_down", [16, 128, NJ * 128], F32, kind="ExternalInput").ap()
    vecs = nc.dram_tensor("vecs", [128, NVEC], F32, kind="ExternalInput").ap()
    xo = nc.dram_tensor("xo", [16, 128, NT], F32, kind="ExternalOutput").ap()
    mixv = mixT.rearrange("kc p n -> p kc n")
    xv = xT.rearrange("kc p n -> p kc n")
    xov = xo.rearrange("kc p n -> p kc n")
    with ExitStack() as es:
        P = Prog(nc, es)
        xo_b = P.view(xo)
        banks = [P.ps("bank%d" % i, [128, 512]) for i in range(8)]
        vc = P.sb("vecs_s", [128, NVEC])
        P.dma("sp", vc[:], vecs[:, :], writes=[vc])
        ones_f = vc[:, V_ONES:V_ONES + 128]
        wglu_s = P.sb("wglu_s", [128, 2048])
        wglu_b = P.sb("wglu_b", [128, 4, 512], BF16)
        P.dma("act", wglu_s[:], w_glu[:, :], writes=[wglu_s])
        P.op("dve", lambda e: e.tensor_copy(out=wglu_b[:].rearrange("p a b -> p (a b)"), in_=wglu_s[:]), reads=[wglu_s], writes=[wglu_b])
        mt = [P.sb("mt%d" % q, [128, 4, 512]) for q in range(4)]
        xt = [P.sb("xt%d" % q, [128, 4, 512]) for q in range(4)]
        mixb = [P.sb("mixb%d" % q, [128, 4, 512], BF16) for q in range(4)]
        act = [P.sb("act%d" % q, [128, 4, 512], BF16) for q in range(11)]
        NSTG = 2
        stg = [P.sb("stg%d" % i, [128, 2048]) for i in range(NSTG)]
        wbf = [P.sb("wbf%d" % i, [128, 2048], BF16) for i in range(NSTG)]
        sqb = [P.sb("csq%d" % i, [128, 512]) for i in range(2)]
        sd = P.sb("csd", [128, 512])
        rstd = P.sb("crstd", [128, 512])
        tA = [P.sb("ctA%d" % i, [128, 512]) for i in range(2)]
        tB = [P.sb("ctB%d" % i, [128, 512]) for i in range(2)]
        gf = [P.sb("cgf%d" % i, [128, 512]) for i in range(4)]
        gb = [P.sb("cgb%d" % i, [128, 512], BF16) for i in range(4)]
        so = gf
        sg = [P.sb("csg%d" % i, [128, 512]) for i in range(2)]
        wcnt = [0]
        bcnt = [0]

        def load_weight(src_ap, n):
            i = wcnt[0] % NSTG
            wcnt[0] += 1
            s_, b_ = stg[i], wbf[i]
            P.dma("sp" if wcnt[0] % 2 == 0 else "act", s_[:, 0:n], src_ap, writes=[s_])
            eng = "pool" if wcnt[0] % 2 == 0 else "dve"
            P.op(eng, lambda e: e.tensor_copy(out=b_[:, 0:n], in_=s_[:, 0:n]), reads=[s_], writes=[b_])
            return b_

        def rstd_of(chunks, w, D):
            st = banks[6]
            n = len(chunks)
            for i, (ap, bufs) in enumerate(chunks):
                sq = sqb[i % 2]
                P.op("act", lambda e, sq=sq, ap=ap: e.activation(out=sq[:, :w], in_=ap, func=AF.Square), reads=bufs, writes=[sq])
                P.op("pe", lambda e, sq=sq, i=i: e.matmul(st[:, :w], lhsT=ones_f, rhs=sq[:, :w], start=(i == 0), stop=(i == n - 1)), reads=[sq, vc], writes=[st])
            P.op("act", lambda e: e.activation(out=sd[:, :w], in_=st[:, :w], func=AF.Sqrt, scale=1.0 / D, bias=EPS_C), reads=[st], writes=[sd])
            P.op("dve", lambda e: e.reciprocal(out=rstd[:, :w], in_=sd[:, :w]), reads=[sd], writes=[rstd])
            return rstd

        for (t0, w) in tiles:
            for q in range(4):
                P.dma("sp", mt[q][:, :, :w], mixv[:, 4 * q:4 * q + 4, t0:t0 + w], writes=[mt[q]])
                P.dma("act", xt[q][:, :, :w], xv[:, 4 * q:4 * q + 4, t0:t0 + w], writes=[xt[q]])
            for c in range(4):
                y = mt[2][:, c, :w]
                a, b_ = tA[c % 2], tB[c % 2]
                P.op("pool", lambda e, a=a, y=y: e.tensor_tensor(out=a[:, :w], in0=y, in1=y, op=ALU.mult), reads=[mt[2]], writes=[a])
                P.op("dve", lambda e, a=a: e.tensor_scalar(out=a[:, :w], in0=a[:, :w], scalar1=0.044715, scalar2=1.0, op0=ALU.mult, op1=ALU.add), reads=[a], writes=[a])
                P.op("pool", lambda e, a=a, b_=b_, y=y: e.tensor_tensor(out=b_[:, :w], in0=a[:, :w], in1=y, op=ALU.mult), reads=[a, mt[2]], writes=[b_])
                P.op("act", lambda e, a=a, b_=b_: e.activation(out=a[:, :w], in_=b_[:, :w], func=AF.Sigmoid, scale=1.5957691216057308), reads=[b_], writes=[a])
                P.op("pool", lambda e, a=a, y=y, c=c: e.tensor_tensor(out=gf[c][:, :w], in0=a[:, :w], in1=y, op=ALU.mult), reads=[a, mt[2]], writes=[gf[c]])
                P.op("dve", lambda e, c=c: e.tensor_copy(out=gb[c][:, :w], in_=gf[c][:, :w]), reads=[gf[c]], writes=[gb[c]])
            for oc in range(4):
                ps = banks[7]
                for c in range(4):
                    P.op("pe", lambda e, c=c, oc=oc: e.matmul(ps[:, :w], lhsT=wglu_b[:, c, oc * 128:(oc + 1) * 128], rhs=gb[c][:, :w], start=(c == 0), stop=(c == 3)),
                         reads=[wglu_b, gb[c]], writes=[ps])
                a = tA[oc % 2]
                P.op("act", lambda e, a=a, oc=oc: e.activation(out=a[:, :w], in_=ps[:, :w], func=AF.Sigmoid, bias=vc[:, V_BGLU + oc:V_BGLU + oc + 1], scale=1.0),
                     reads=[ps, vc], writes=[a])
                P.op("dve", lambda e, a=a, oc=oc: e.tensor_tensor(out=so[oc][:, :w], in0=gf[oc][:, :w], in1=a[:, :w], op=ALU.mult), reads=[gf[oc], a], writes=[so[oc]])
            r = rstd_of([(so[c][:, :w], [so[c]]) for c in range(4)], w, 512)
            for c in range(4):
                P.op("dve", lambda e, c=c: e.scalar_tensor_tensor(out=mixb[2][:, c, :w], in0=so[c][:, :w], scalar=vc[:, V_SSMN + c:V_SSMN + c + 1], in1=r[:, :w],
                                                              op0=ALU.mult, op1=ALU.mult), reads=[so[c], vc, r], writes=[mixb[2]])
            for q in (0, 1, 3):
                eng = "pool" if q % 2 == 0 else "dve"
                P.op(eng, lambda e, q=q: e.tensor_copy(out=mixb[q][:, :, :w], in_=mt[q][:, :, :w]), reads=[mt[q]], writes=[mixb[q]])
            for oc in range(16):
                wb = load_weight(w_out[oc, :, :], 2048)
                ps = banks[bcnt[0] % 2]
                bcnt[0] += 1
                for kc in range(16):
                    P.op("pe", lambda e, ps=ps, wb=wb, kc=kc: e.matmul(ps[:, :w], lhsT=wb[:, kc * 128:(kc + 1) * 128], rhs=mixb[kc // 4][:, kc % 4, :w],
                                                                   start=(kc == 0), stop=(kc == 15)), reads=[wb, mixb[kc // 4]], writes=[ps])
                P.op("act", lambda e, ps=ps, oc=oc: e.activation(out=mt[oc // 4][:, oc % 4, :w], in_=ps[:, :w], func=AF.Copy), reads=[ps], writes=[mt[oc // 4]])

            def post_norm_residual(gcol0):
                r = rstd_of([(mt[kc // 4][:, kc % 4, :w], [mt[kc // 4]]) for kc in range(16)], w, 2048)
                for kc in range(16):
                    m_, x_ = mt[kc // 4], xt[kc // 4]
                    P.op("dve", lambda e, m_=m_, kc=kc: e.scalar_tensor_tensor(out=m_[:, kc % 4, :w], in0=m_[:, kc % 4, :w], scalar=vc[:, gcol0 + kc:gcol0 + kc + 1], in1=r[:, :w],
                                                                          op0=ALU.mult, op1=ALU.mult), reads=[m_, vc, r], writes=[m_])
                    P.op("pool", lambda e, m_=m_, x_=x_, kc=kc: e.tensor_tensor(out=x_[:, kc % 4, :w], in0=x_[:, kc % 4, :w], in1=m_[:, kc % 4, :w], op=ALU.add),
                         reads=[m_, x_], writes=[x_])

            post_norm_residual(V_GPOST)
            r = rstd_of([(xt[kc // 4][:, kc % 4, :w], [xt[kc // 4]]) for kc in range(16)], w, 2048)
            for kc in range(16):
                P.op("dve", lambda e, kc=kc: e.scalar_tensor_tensor(out=mixb[kc // 4][:, kc % 4, :w], in0=xt[kc // 4][:, kc % 4, :w], scalar=vc[:, V_GFPRE + kc:V_GFPRE + kc + 1],
                                                                in1=r[:, :w], op0=ALU.mult, op1=ALU.mult), reads=[xt[kc // 4], vc, r], writes=[mixb[kc // 4]])
            for j in range(NJ):
                wg = load_weight(w_gate[j, :, :], 2048)
                wu = load_weight(w_up[j, :, :], 2048)
                pg = banks[2 + j % 2]
                pu = banks[4 + j % 2]
                for kc in range(16):
                    P.op("pe", lambda e, pg=pg, wg=wg, kc=kc: e.matmul(pg[:, :w], lhsT=wg[:, kc * 128:(kc + 1) * 128], rhs=mixb[kc // 4][:, kc % 4, :w],
                                                                   start=(kc == 0), stop=(kc == 15)), reads=[wg, mixb[kc // 4]], writes=[pg])
                for kc in range(16):
                    P.op("pe", lambda e, pu=pu, wu=wu, kc=kc: e.matmul(pu[:, :w], lhsT=wu[:, kc * 128:(kc + 1) * 128], rhs=mixb[kc // 4][:, kc % 4, :w],
                                                                   start=(kc == 0), stop=(kc == 15)), reads=[wu, mixb[kc // 4]], writes=[pu])
                s_ = sg[j % 2]
                P.op("act", lambda e, s_=s_, pg=pg: e.activation(out=s_[:, :w], in_=pg[:, :w], func=AF.Silu), reads=[pg], writes=[s_])
                P.op("dve", lambda e, s_=s_, pu=pu, j=j: e.tensor_tensor(out=act[j // 4][:, j % 4, :w], in0=s_[:, :w], in1=pu[:, :w], op=ALU.mult),
                     reads=[s_, pu], writes=[act[j // 4]])
            for oc in range(16):
                ps = banks[bcnt[0] % 2]
                bcnt[0] += 1
                for half in range(4):
                    wd = load_weight(w_down[oc, :, half * 1408:(half + 1) * 1408], 1408)
                    for kk in range(11):
                        j = half * 11 + kk
                        P.op("pe", lambda e, ps=ps, wd=wd, kk=kk, j=j: e.matmul(ps[:, :w], lhsT=wd[:, kk * 128:(kk + 1) * 128], rhs=act[j // 4][:, j % 4, :w],
                                                                            start=(j == 0), stop=(j == NJ - 1)), reads=[wd, act[j // 4]], writes=[ps])
                P.op("act", lambda e, ps=ps, oc=oc: e.activation(out=mt[oc // 4][:, oc % 4, :w], in_=ps[:, :w], func=AF.Copy), reads=[ps], writes=[mt[oc // 4]])
            post_norm_residual(V_GFPOST)
            for q in range(4):
                P.dma("sp", xov[:, 4 * q:4 * q + 4, t0:t0 + w], xt[q][:, :, :w], reads=[xt[q]], writes=[xo_b])
        P.finish([xo_b])
        print("C program: ins=%d waits=%d" % (P.n_ins, P.n_wait))
    return nc


def prep_c_weights(inp, l):
    wo = inp['w_out'][l].reshape(16, 128, 16, 128).transpose(2, 1, 0, 3).reshape(16, 128, 2048)
    wgl = inp['ssm_w_glu'][l].reshape(4, 128, 512).transpose(1, 0, 2).reshape(128, 2048)
    wga = inp['w_ffn_gate'][l].reshape(16, 128, NJ, 128).transpose(2, 1, 0, 3).reshape(NJ, 128, 2048)
    wup = inp['w_ffn_up'][l].reshape(16, 128, NJ, 128).transpose(2, 1, 0, 3).reshape(NJ, 128, 2048)
    wdn = inp['w_ffn_down'][l].reshape(NJ, 128, 16, 128).transpose(2, 1, 0, 3).reshape(16, 128, NJ * 128)
    v = np.zeros((128, NVEC), np.float32)
    v[:, V_GPOST:V_GPOST + 16] = inp['norm_mix_post'][l].reshape(16, 128).T
    v[:, V_GFPRE:V_GFPRE + 16] = inp['norm_ffn_pre'][l].reshape(16, 128).T
    v[:, V_GFPOST:V_GFPOST + 16] = inp['norm_ffn_post'][l].reshape(16, 128).T
    v[:, V_SSMN:V_SSMN + 4] = inp['ssm_norm'][l].reshape(4, 128).T
    v[:, V_BGLU:V_BGLU + 4] = inp['ssm_b_glu'][l].reshape(4, 128).T
    v[:, V_ONES:V_ONES + 128] = 1.0
    c = np.ascontiguousarray
    return {"w_out": c(wo), "w_glu": c(wgl), "w_gate": c(wga), "w_up": c(wup), "w_down": c(wdn), "vecs": v}


def kernel(**inputs):
    inp = {k: np.asarray(v) for k, v in inputs.items()}
    x = inp['x'].astype(np.float32)
    B, SEQ, D = x.shape
    Lt = SEQ + 16
    NS = PADF + Lt
    h = np.concatenate([np.broadcast_to(inp['meta_tokens'][None].astype(np.float32), (B, 16, D)), x], axis=1)
    tabs = make_tabs(NS)
    per = SEQ // 4
    tilesC = []
    t = 0
    while t < per:
        tilesC.append((t, min(512, per - t)))
        t += 512
    tilesC.append((per, 16))
    NT = per + 16
    ncC = build_c(tilesC)
    for l in range(2):
        ncA = build_ab(NS, l)
        in_maps = [prep_ab(inp, l, c // 4, c % 4, h, NS, tabs) for c in range(8)]
        res = run_bass_kernel_spmd(ncA, in_maps, core_ids=list(range(8)))
        mq = [np.asarray(res.results[c]["mixq"]) for c in range(8)]
        del in_maps
        cw = prep_c_weights(inp, l)
        in_maps = []
        for c in range(8):
            b, q = c // 4, c % 4
            sl = np.concatenate([np.arange(128 + q * per, 128 + (q + 1) * per), np.arange(PADF, 128)])
            mixT = np.empty((16, 128, NT), np.float32)
            for grp in range(4):
                for qq in range(4):
                    mixT[grp * 4 + qq] = mq[b * 4 + qq][grp][:, sl]
            xT = np.ascontiguousarray(h[b][sl - PADF].T).reshape(16, 128, NT)
            d = {"mixT": mixT, "xT": xT}
            d.update(cw)
            in_maps.append(d)
        res = run_bass_kernel_spmd(ncC, in_maps, core_ids=list(range(8)))
        del in_maps
        hn = np.empty_like(h)
        for c in range(8):
            b, q = c // 4, c % 4
            xo = np.asarray(res.results[c]["xo"]).reshape(D, NT).T
            hn[b, 16 + q * per:16 + (q + 1) * per] = xo[:per]
            if q == 0:
                hn[b, 0:16] = xo[per:]
        h = hn
    return np.ascontiguousarray(h[:, 16:]).astype(np.float32)
```
